# Optimizing a Trainium2 kernel written in Bass

```python
import jax, jax.numpy as jnp
from jax import lax
import numpy as np

D_MODEL = 1024
BATCH = 2
SEQ = 8192
DEPTH = 2

CTX_LEN = 256
GRID_W = 64
ROPE_THETA = 10000.0
NORM_EPS = 1e-6

RET_HEADS = 4
RET_DK = 32
RET_DV = 64
RET_CHUNK = 128
RET_WIDTH = RET_HEADS * RET_DV
MLA_HEADS = 8
MLA_Q_RANK = 384
MLA_KV_RANK = 256
MLA_NOPE = 64
MLA_ROPE = 32
MLA_DV = 64
MLA_SCALE = (MLA_NOPE + MLA_ROPE) ** -0.5
ATTN_BLOCK = 128
MLA_WIDTH = MLA_HEADS * MLA_DV
POOL_WINDOWS = (2, 4, 8, 16)
POOL_GROUP = 64
POOL_WIDTH = 4 * POOL_GROUP

MIX_WIDTH = RET_WIDTH + MLA_WIDTH + POOL_WIDTH
IN_SECTIONS = (RET_HEADS * RET_DK, RET_HEADS * RET_DK, RET_WIDTH, RET_WIDTH,
               MLA_Q_RANK, MLA_KV_RANK, MLA_ROPE, POOL_WIDTH)
IN_WIDTH = 2 * RET_HEADS * RET_DK + 2 * RET_WIDTH + MLA_Q_RANK + MLA_KV_RANK + MLA_ROPE + POOL_WIDTH

D_FF = 2816
CONV_WIDTH = 3

kernel_name = 'hybrid_ret_mla_pool_prefix_dit'


def rms_norm(x, g):
    xf = x.astype(jnp.float32)
    y = xf * lax.rsqrt(jnp.mean(xf * xf, axis=-1, keepdims=True) + NORM_EPS)
    return (y * g.astype(jnp.float32)).astype(x.dtype)


def modulate(h, shift, scale):
    return h * (1.0 + scale) + shift


def split_in_proj(p):
    offs = []
    acc = 0
    for s in IN_SECTIONS[:-1]:
        acc += s
        offs.append(acc)
    return jnp.split(p, offs, axis=-1)


def split_heads(t, n_heads):
    B, T, _ = t.shape
    return t.reshape(B, T, n_heads, -1).transpose(0, 2, 1, 3)


def merge_heads(t):
    B, H, T, d = t.shape
    return t.transpose(0, 2, 1, 3).reshape(B, T, H * d)


def axial_rope_tables(rows, rot_dim):
    row = jnp.repeat(jnp.arange(rows, dtype=jnp.float32), GRID_W)
    col = jnp.tile(jnp.arange(GRID_W, dtype=jnp.float32), rows)
    n_freq = rot_dim // 4
    inv = ROPE_THETA ** (-jnp.arange(n_freq, dtype=jnp.float32) / n_freq)
    ang = jnp.concatenate([row[:, None] * inv, col[:, None] * inv], axis=-1)
    return jnp.cos(ang), jnp.sin(ang)


def apply_rope(x, cos, sin):
    xf = x.astype(jnp.float32)
    h = xf.shape[-1] // 2
    x1, x2 = xf[..., :h], xf[..., h:]
    return jnp.concatenate([x1 * cos - x2 * sin, x2 * cos + x1 * sin], axis=-1).astype(x.dtype)


def retention_chunkwise(q, k, v, log_g, s0, include_diag):
    B, H, T, _ = q.shape
    dv = v.shape[-1]
    n = T // RET_CHUNK
    idx = jnp.arange(RET_CHUNK, dtype=jnp.float32)
    diff = idx[:, None] - idx[None, :]
    mask = diff >= (0.0 if include_diag else 1.0)
    dmat = jnp.where(mask, jnp.exp(log_g[:, None, None] * jnp.maximum(diff, 0.0)), 0.0)
    q_dec = jnp.exp(log_g[:, None] * (idx + 1.0))[:, :, None]
    k_dec = jnp.exp(log_g[:, None] * (RET_CHUNK - 1.0 - idx))[:, :, None]
    c_dec = jnp.exp(log_g * RET_CHUNK)[:, None, None]

    def chunks(t):
        return jnp.moveaxis(t.reshape(B, H, n, RET_CHUNK, t.shape[-1]), 2, 0)

    def step(s, inp):
        qi, ki, vi = inp
        a = jnp.einsum('bhqd,bhkd->bhqk', qi, ki) * dmat
        o = jnp.einsum('bhqk,bhkv->bhqv', a, vi) + jnp.einsum('bhqd,bhdv->bhqv', qi * q_dec, s)
        s = s * c_dec + jnp.einsum('bhkd,bhkv->bhdv', ki * k_dec, vi)
        return s, o

    _, o = lax.scan(step, s0, (chunks(q), chunks(k), chunks(v)))
    return jnp.moveaxis(o, 0, 2).reshape(B, H, T, dv)


def retention_mixer(lat, ctx, decay_f, decay_b, cos, sin, with_ctx_out):
    log_gf = jax.nn.log_sigmoid(decay_f.astype(jnp.float32))
    log_gb = jax.nn.log_sigmoid(decay_b.astype(jnp.float32))

    def ret_q(q, rotate):
        q = split_heads(q, RET_HEADS).astype(jnp.float32)
        return apply_rope(q, cos, sin) if rotate else q

    def ret_kv(k, v, rotate):
        k = split_heads(k, RET_HEADS).astype(jnp.float32) * (RET_DK ** -0.5)
        if rotate:
            k = apply_rope(k, cos, sin)
        return k, split_heads(v, RET_HEADS).astype(jnp.float32)

    def bidir(q, k, v, s_f, s_b):
        fwd = retention_chunkwise(q, k, v, log_gf, s_f, True)
        bwd = retention_chunkwise(jnp.flip(q, 2), jnp.flip(k, 2), jnp.flip(v, 2), log_gb, s_b, False)
        return fwd + jnp.flip(bwd, 2)

    def finish(o, g):
        o = o * lax.rsqrt(jnp.mean(o * o, axis=-1, keepdims=True) + NORM_EPS)
        return (merge_heads(o) * jax.nn.silu(g.astype(jnp.float32))).astype(g.dtype)

    kc, vc = ret_kv(ctx[1], ctx[2], False)
    tc = kc.shape[2]
    m = jnp.arange(tc, dtype=jnp.float32)
    s_f = jnp.einsum('bhmd,bhme->bhde', kc * jnp.exp(log_gf[:, None] * (tc - 1.0 - m))[:, :, None], vc)
    s_b = jnp.einsum('bhmd,bhme->bhde', kc * jnp.exp(log_gb[:, None] * m)[:, :, None], vc)

    ql = ret_q(lat[0], True)
    kl, vl = ret_kv(lat[1], lat[2], True)
    y_lat = finish(bidir(ql, kl, vl, s_f, s_b), lat[3])
    y_ctx = None
    if with_ctx_out:
        z = jnp.zeros_like(s_f)
        y_ctx = finish(bidir(ret_q(ctx[0], False), kc, vc, z, z), ctx[3])
    return y_lat, y_ctx


def mla_attend(qn, qr, kn, kr, v):
    s = jnp.einsum('bhqd,bhkd->bhqk', qn, kn) + jnp.einsum('bhqr,bkr->bhqk', qr, kr)
    p = jax.nn.softmax(s.astype(jnp.float32) * MLA_SCALE, axis=-1)
    return jnp.einsum('bhqk,bhkd->bhqd', p.astype(v.dtype), v)


def mla_mixer(lat, ctx, q_norm_g, w_uq, kv_norm_g, w_ukv, cos, sin, with_ctx_out):
    def queries(cq, rotate):
        B, T, _ = cq.shape
        q = (rms_norm(cq, q_norm_g) @ w_uq).reshape(B, T, MLA_HEADS, MLA_NOPE + MLA_ROPE).transpose(0, 2, 1, 3)
        qn, qr = q[..., :MLA_NOPE], q[..., MLA_NOPE:]
        return qn, (apply_rope(qr, cos, sin) if rotate else qr)

    def keys_values(ckv, kr, rotate):
        B, T, _ = ckv.shape
        kv = (rms_norm(ckv, kv_norm_g) @ w_ukv).reshape(B, T, MLA_HEADS, MLA_NOPE + MLA_DV).transpose(0, 2, 1, 3)
        kr = apply_rope(kr, cos, sin) if rotate else kr
        return kv[..., :MLA_NOPE], kr, kv[..., MLA_NOPE:]

    kn_c, kr_c, v_c = keys_values(ctx[1], ctx[2], False)
    kn_l, kr_l, v_l = keys_values(lat[1], lat[2], True)
    kn = jnp.concatenate([kn_c, kn_l], axis=2)
    kr = jnp.concatenate([kr_c, kr_l], axis=1)
    v = jnp.concatenate([v_c, v_l], axis=2)

    qn, qr = queries(lat[0], True)
    B, H, T, _ = qn.shape
    nb = T // ATTN_BLOCK

    def blocks(t):
        return jnp.moveaxis(t.reshape(B, H, nb, ATTN_BLOCK, t.shape[-1]), 2, 0)

    o = lax.map(lambda qb: mla_attend(qb[0], qb[1], kn, kr, v), (blocks(qn), blocks(qr)))
    y_lat = merge_heads(jnp.moveaxis(o, 0, 2).reshape(B, H, T, MLA_DV))
    y_ctx = None
    if with_ctx_out:
        qn_c, qr_c = queries(ctx[0], False)
        y_ctx = merge_heads(mla_attend(qn_c, qr_c, kn_c, kr_c, v_c))
    return y_lat, y_ctx


def pool_mixer(x, pool_w, pool_scale):
    B, T, _ = x.shape
    xf = x.astype(jnp.float32)
    cs = jnp.pad(jnp.cumsum(xf, axis=1), ((0, 0), (1, 0), (0, 0)))
    t = jnp.arange(T)
    parts = []
    for gi, w in enumerate(POOL_WINDOWS):
        sl = slice(gi * POOL_GROUP, (gi + 1) * POOL_GROUP)
        lo = jnp.clip(t - w // 2, 0, T)
        hi = jnp.clip(t + w - w // 2, 0, T)
        csg = cs[:, :, sl]
        mean = (csg[:, hi] - csg[:, lo]) / (hi - lo).astype(jnp.float32)[None, :, None]
        parts.append(mean - xf[:, :, sl])
    d = jnp.stack(parts, axis=2)
    y = jnp.einsum('btgc,gcd->btgd', d, pool_w.astype(jnp.float32)).reshape(B, T, POOL_WIDTH)
    return (y * pool_scale.astype(jnp.float32)).astype(x.dtype)


def conv_ffn(h, w_up, conv_w, conv_b, w_down):
    u = h @ w_up
    up = jnp.pad(u, ((0, 0), (1, 1), (0, 0)))
    u = up[:, :-2] * conv_w[0] + up[:, 1:-1] * conv_w[1] + up[:, 2:] * conv_w[2] + conv_b
    a, b = jnp.split(u, 2, axis=-1)
    return (jax.nn.silu(a) * b) @ w_down


def trunk_layer(x, xc, mod, mod_c, norm1_g, w_in, ret_decay_f, ret_decay_b, mla_q_norm_g, w_uq,
                mla_kv_norm_g, w_ukv, pool_w, pool_scale, w_out, norm2_g, w_up, conv_w, conv_b, w_down,
                cos_r, sin_r, cos_m, sin_m, with_ctx_out):
    sh1, sc1, g1, sh2, sc2, g2 = jnp.split(mod, 6, axis=-1)
    sh1c, sc1c, g1c, sh2c, sc2c, g2c = jnp.split(mod_c, 6, axis=-1)
    p = split_in_proj(modulate(rms_norm(x, norm1_g), sh1, sc1) @ w_in)
    pc = split_in_proj(modulate(rms_norm(xc, norm1_g), sh1c, sc1c) @ w_in)

    y_ret, y_ret_c = retention_mixer((p[0], p[1], p[2], p[3]), (pc[0], pc[1], pc[2], pc[3]),
                                     ret_decay_f, ret_decay_b, cos_r, sin_r, with_ctx_out)
    y_mla, y_mla_c = mla_mixer((p[4], p[5], p[6]), (pc[4], pc[5], pc[6]), mla_q_norm_g, w_uq,
                               mla_kv_norm_g, w_ukv, cos_m, sin_m, with_ctx_out)
    y_pool = pool_mixer(p[7], pool_w, pool_scale)

    x = x + g1 * (jnp.concatenate([y_ret, y_mla, y_pool], axis=-1) @ w_out)
    x = x + g2 * conv_ffn(modulate(rms_norm(x, norm2_g), sh2, sc2), w_up, conv_w, conv_b, w_down)
    if with_ctx_out:
        y_pool_c = pool_mixer(pc[7], pool_w, pool_scale)
        xc = xc + g1c * (jnp.concatenate([y_ret_c, y_mla_c, y_pool_c], axis=-1) @ w_out)
        xc = xc + g2c * conv_ffn(modulate(rms_norm(xc, norm2_g), sh2c, sc2c), w_up, conv_w, conv_b, w_down)
    return x, xc


def setup_inputs(seed: int = 0) -> dict:
    key = jax.random.key(seed)
    ks = jax.random.split(key, 24)
    f32 = jnp.float32
    L = DEPTH

    def nrm(k, shape, scale):
        return jax.random.normal(k, shape, f32) * scale

    def gain(k, shape):
        return 1.0 + 0.02 * jax.random.normal(k, shape, f32)

    heads = jnp.arange(RET_HEADS, dtype=f32)
    decay_logit = jnp.log(jnp.exp2(5.0 + heads) - 1.0)
    return {
        'x': nrm(ks[0], (BATCH, SEQ, D_MODEL), 1.0),
        'c': nrm(ks[1], (BATCH, D_MODEL), 1.0),
        'ctx': nrm(ks[2], (BATCH, CTX_LEN, D_MODEL), 1.0),
        'c_ctx': nrm(ks[3], (D_MODEL,), 1.0),
        'w_mod': nrm(ks[4], (L, D_MODEL, 6 * D_MODEL), 0.5 * D_MODEL ** -0.5),
        'b_mod': nrm(ks[5], (L, 6 * D_MODEL), 0.01),
        'norm1_g': gain(ks[6], (L, D_MODEL)),
        'w_in': nrm(ks[7], (L, D_MODEL, IN_WIDTH), D_MODEL ** -0.5),
        'ret_decay_f': decay_logit + nrm(ks[8], (L, RET_HEADS), 0.1),
        'ret_decay_b': decay_logit + nrm(ks[9], (L, RET_HEADS), 0.1),
        'mla_q_norm_g': gain(ks[10], (L, MLA_Q_RANK)),
        'w_uq': nrm(ks[11], (L, MLA_Q_RANK, MLA_HEADS * (MLA_NOPE + MLA_ROPE)), MLA_Q_RANK ** -0.5),
        'mla_kv_norm_g': gain(ks[12], (L, MLA_KV_RANK)),
        'w_ukv': nrm(ks[13], (L, MLA_KV_RANK, MLA_HEADS * (MLA_NOPE + MLA_DV)), MLA_KV_RANK ** -0.5),
        'pool_w': nrm(ks[14], (L, len(POOL_WINDOWS), POOL_GROUP, POOL_GROUP), POOL_GROUP ** -0.5),
        'pool_scale': gain(ks[15], (L, POOL_WIDTH)),
        'w_out': nrm(ks[16], (L, MIX_WIDTH, D_MODEL), MIX_WIDTH ** -0.5),
        'norm2_g': gain(ks[17], (L, D_MODEL)),
        'w_up': nrm(ks[18], (L, D_MODEL, 2 * D_FF), D_MODEL ** -0.5),
        'conv_w': nrm(ks[19], (L, CONV_WIDTH, 2 * D_FF), CONV_WIDTH ** -0.5),
        'conv_b': nrm(ks[20], (L, 2 * D_FF), 0.01),
        'w_down': nrm(ks[21], (L, D_FF, D_MODEL), D_FF ** -0.5),
        'final_norm_g': gain(ks[22], (D_MODEL,)),
    }


def reference(x, c, ctx, c_ctx, w_mod, b_mod, norm1_g, w_in, ret_decay_f, ret_decay_b, mla_q_norm_g, w_uq,
              mla_kv_norm_g, w_ukv, pool_w, pool_scale, w_out, norm2_g, w_up, conv_w, conv_b, w_down,
              final_norm_g):
    n_tokens = x.shape[1]
    ROWS = n_tokens // GRID_W
    cos_r, sin_r = axial_rope_tables(ROWS, RET_DK)
    cos_m, sin_m = axial_rope_tables(ROWS, MLA_ROPE)
    sc = jax.nn.silu(c)
    sc_ctx = jax.nn.silu(c_ctx)
    xc = ctx
    for l in range(DEPTH):
        mod = (sc @ w_mod[l] + b_mod[l])[:, None, :]
        mod_c = (sc_ctx @ w_mod[l] + b_mod[l])[None, None, :]
        x, xc = trunk_layer(x, xc, mod, mod_c, norm1_g[l], w_in[l], ret_decay_f[l], ret_decay_b[l],
                            mla_q_norm_g[l], w_uq[l], mla_kv_norm_g[l], w_ukv[l], pool_w[l], pool_scale[l],
                            w_out[l], norm2_g[l], w_up[l], conv_w[l], conv_b[l], w_down[l],
                            cos_r, sin_r, cos_m, sin_m, l < DEPTH - 1)
    return rms_norm(x, final_norm_g)
```

```python
import numpy as np
from contextlib import ExitStack
import concourse.bass as bass
import concourse.mybir as mybir
from concourse.bass_utils import run_bass_kernel_spmd

F32 = mybir.dt.float32
BF16 = mybir.dt.bfloat16
U8 = mybir.dt.uint8
AF = mybir.ActivationFunctionType
ALU = mybir.AluOpType
AX = mybir.AxisListType

NCORES = 8
D = 1024
KC = 8
TL = 2048
TC = 256
DEPTH = 2
DFF = 2816
NG = 22
INW = 1696
EPS = 1e-6
MLA_SCALE = 96 ** -0.5
RET_KSCALE = 32 ** -0.5
NKEY = TC + 4 * TL


class _Op:
    __slots__ = ("q", "fn", "kind", "raw", "oth", "waits", "sig", "dslot", "dval", "dprev", "idx")


QS = ("pe", "act", "dve", "pool", "sp")
NDS = 8


class Sched:
    def __init__(self):
        self.ops = []
        self.lw = {}
        self.rdc = {}
        self.rdd = {}

    @staticmethod
    def keys(items):
        out = []
        for x in items:
            if x is None:
                continue
            if isinstance(x, (str, tuple)):
                out.append(x)
                continue
            sp = str(x.space)
            dims = x.ap
            esz = mybir.dt.size(x.dtype)
            pstr = dims[0][0]
            off = x.offset % pstr if pstr > 0 else x.offset
            ext = 1 + sum((c - 1) * st for st, c in dims[1:])
            lo = off * esz
            hi = (off + ext) * esz
            blk = 2048 if "PSUM" in sp else 512
            nm = x.name
            out.extend((nm, b) for b in range(lo // blk, (hi - 1) // blk + 1))
        return out

    def add(self, q, fn, reads=(), writes=(), kind="c"):
        op = _Op()
        op.q = q
        op.fn = fn
        op.kind = kind
        op.idx = len(self.ops)
        op.sig = 0
        rk = self.keys(reads)
        wk = self.keys(writes)
        psr = [k for k in rk if isinstance(k, tuple) and k[0] in ("ps", "pT")]
        if psr:
            wk = wk + [k for k in psr if k not in wk]
        raw = set()
        oth = set()
        for k in rk:
            w = self.lw.get(k)
            if w is not None:
                raw.add(w)
        for k in wk:
            w = self.lw.get(k)
            if w is not None:
                oth.add(w)
            d = self.rdc.get(k)
            if d:
                oth.update(d.values())
            l = self.rdd.get(k)
            if l:
                oth.update(l)
        for k in rk:
            if kind == "c":
                self.rdc.setdefault(k, {})[q] = op.idx
            else:
                l = self.rdd.setdefault(k, [])
                l.append(op.idx)
                if len(l) > 24:
                    del l[0]
        for k in wk:
            self.lw[k] = op.idx
            self.rdc[k] = {}
            self.rdd[k] = []
        raw.discard(op.idx)
        oth.discard(op.idx)
        oth -= raw
        op.raw = raw
        op.oth = oth
        self.ops.append(op)
        return op

    def _needs(self, op, d):
        dop = self.ops[d]
        if dop.kind != "c":
            return True
        if dop.q == op.q and op.kind == "c" and op.q == "pe":
            return False
        return True

    def finalize(self):
        need = set()
        for op in self.ops:
            for d in op.raw | op.oth:
                if self.ops[d].kind == "c" and self._needs(op, d):
                    need.add(d)
        cnt = {q: 0 for q in QS}
        dcnt = {q: 0 for q in QS}
        ccn = 0
        for op in self.ops:
            if op.kind == "c":
                if op.idx in need:
                    cnt[op.q] += 1
                    op.sig = cnt[op.q]
            elif op.kind == "d":
                i = dcnt[op.q]
                dcnt[op.q] += 1
                op.dslot = i % NDS
                op.dval = 16 * (i // NDS + 1)
                op.dprev = 16 * (i // NDS)
            else:
                ccn += 1
                op.dval = ccn
        waited = {q: {} for q in QS}
        for op in self.ops:
            w = {}
            wq = waited[op.q]
            for d in op.raw | op.oth:
                if not self._needs(op, d):
                    continue
                dop = self.ops[d]
                if dop.kind == "c":
                    key, val = ("c", dop.q), dop.sig
                elif dop.kind == "d":
                    key, val = ("d", dop.q, dop.dslot), dop.dval
                else:
                    key, val = ("cc",), dop.dval
                if wq.get(key, 0) < val and w.get(key, 0) < val:
                    w[key] = val
            if op.kind == "d" and op.dprev:
                key = ("d", op.q, op.dslot)
                if wq.get(key, 0) < op.dprev and w.get(key, 0) < op.dprev:
                    w[key] = op.dprev
            wq.update(w)
            op.waits = list(w.items())
        self.nsig = cnt

    def emit(self, nc):
        self.finalize()
        with ExitStack() as st:
            sems = {}
            for q in ("pe", "act", "dve", "pool"):
                sems[("c", q)] = st.enter_context(nc.semaphore("c_" + q))
            for q in ("sp", "pool", "act"):
                for i in range(NDS):
                    sems[("d", q, i)] = st.enter_context(nc.semaphore("d_%s_%d" % (q, i)))
            sems[("cc",)] = st.enter_context(nc.semaphore("ccs"))
            block = st.enter_context(nc.Block())
            byq = {q: [op for op in self.ops if op.q == q] for q in QS}

            def run(q, e):
                for op in byq[q]:
                    for key, val in op.waits:
                        e.wait_ge(sems[key], val)
                    if op.fn is None:
                        continue
                    ins = op.fn(e)
                    if op.kind == "c":
                        if op.sig:
                            ins.then_inc(sems[("c", q)], 1)
                    elif op.kind == "d":
                        ins.then_inc(sems[("d", q, op.dslot)], 16)
                    else:
                        ins.then_inc(sems[("cc",)])

            @block.tensor
            def _(e):
                run("pe", e)

            @block.scalar
            def _(e):
                run("act", e)

            @block.vector
            def _(e):
                run("dve", e)

            @block.gpsimd
            def _(e):
                run("pool", e)

            @block.sync
            def _(e):
                run("sp", e)


def _layout():
    off = {}
    cur = 0

    def f(name, n):
        nonlocal cur
        off[name] = (cur, n)
        cur += n

    f("scin", 16)
    f("bm", 192)
    f("n1g", 16)
    f("n2g", 16)
    f("fng", 8)
    f("qng", 6)
    f("kvng", 4)
    f("psc", 4)
    f("convw", 264)
    f("convb", 88)
    f("dec", 4)
    f("sel", 32)
    f("invw", 2)
    f("poolc", 32)
    f("poolcc", 32)
    f("pcol", 2)
    f("declow", 512)
    f("dpos", 128)
    f("dneg", 128)
    f("mskf", 128)
    f("mskb", 128)
    f("bmask", 256)
    f("posp1", 128)
    f("posm", 128)
    f("c128", 16)
    f("c128r", 16)
    f("c128c", 2)
    f("c128cr", 2)
    return off, cur


CP_OFF, CP_N = _layout()


def _np_layout_vec(v):
    v = np.asarray(v, np.float32)
    lead = v.shape[:-1]
    n = v.shape[-1] // 128
    r = v.reshape(lead + (n, 128))
    r = np.moveaxis(r, -1, 0)
    return np.ascontiguousarray(r)


def build_cpk(core, inp):
    b, j = core // 4, core % 4
    P = np.zeros((128, CP_N), np.float32)

    def put(name, arr):
        o, n = CP_OFF[name]
        arr = np.asarray(arr, np.float32).reshape(128, -1)
        assert arr.shape[1] == n, (name, arr.shape, n)
        P[:, o:o + n] = arr

    sc = np.stack([_np_layout_vec(inp["c"][b]), _np_layout_vec(inp["c_ctx"])], -1)
    put("scin", sc)
    bm = _np_layout_vec(inp["b_mod"].reshape(DEPTH, 6, D))
    put("bm", np.repeat(bm[..., None], 2, -1))
    put("n1g", _np_layout_vec(inp["norm1_g"]))
    put("n2g", _np_layout_vec(inp["norm2_g"]))
    put("fng", _np_layout_vec(inp["final_norm_g"]))
    put("qng", _np_layout_vec(inp["mla_q_norm_g"]))
    put("kvng", _np_layout_vec(inp["mla_kv_norm_g"]))
    put("psc", _np_layout_vec(inp["pool_scale"]))
    put("convw", _np_layout_vec(inp["conv_w"]))
    put("convb", _np_layout_vec(inp["conv_b"]))
    hp = np.arange(128) // 32
    dec = np.stack([inp["ret_decay_f"][:, hp], inp["ret_decay_b"][:, hp]], 1)
    put("dec", np.moveaxis(dec, -1, 0))
    sel = np.zeros(32, np.float32)
    for r in range(4):
        sel[0 + r] = 2048.0 * (j - 1 - r) if r < j else 0.0
        sel[4 + r] = 1.0 if r < j else 0.0
        sel[8 + r] = 2048.0 * (r - j - 1) if r > j else 0.0
        sel[12 + r] = 1.0 if r > j else 0.0
        sel[20 + r] = 1.0 if r == j - 1 else 0.0
        sel[24 + r] = 1.0 if r == j + 1 else 0.0
    sel[16] = 2048.0 * j
    sel[17] = 2048.0 * (3 - j)
    put("sel", np.tile(sel[None], (128, 1)))
    ws = np.array([[2, 4], [8, 16]], np.float32)
    wp = np.stack([ws[c][(np.arange(128) // 64)] for c in range(2)], -1)
    put("invw", 1.0 / wp)

    def poolcorr(T, t0, n):
        out = np.ones((128, 2, 16), np.float32)
        for c in range(2):
            for p in range(128):
                w = int(wp[p, c])
                for i in range(8):
                    for side, tl in ((0, i), (1, n - 8 + i)):
                        t = t0 + tl
                        lo = min(max(t - w // 2, 0), T)
                        hi = min(max(t + w - w // 2, 0), T)
                        out[p, c, side * 8 + i] = w / float(hi - lo)
        return out

    put("poolc", poolcorr(4 * TL, j * TL, TL))
    put("poolcc", poolcorr(TC, 0, TC))
    pp = np.arange(128, dtype=np.float32)
    put("pcol", np.stack([pp, 127.0 - pp], -1))
    hf = np.arange(128) // 32
    declow = np.stack([inp["ret_decay_f"][:, hf], inp["ret_decay_b"][:, hf]], 1)
    put("declow", np.tile(declow.reshape(1, -1), (128, 1)))
    qq = np.arange(128, dtype=np.float32)[None, :]
    kk = np.arange(128, dtype=np.float32)[:, None]
    put("dpos", np.maximum(qq - kk, 0))
    put("dneg", np.maximum(kk - qq, 0))
    put("mskf", (qq >= kk).astype(np.float32))
    put("mskb", (kk > qq).astype(np.float32))
    put("bmask", ((np.arange(128)[:, None] // 32) == (np.arange(256)[None, :] // 64)).astype(np.float32))
    put("posp1", np.tile(qq + 1.0, (128, 1)))
    put("posm", np.tile(128.0 - qq, (128, 1)))
    cc = np.arange(16, dtype=np.float32)
    put("c128", np.tile(128.0 * cc[None], (128, 1)))
    put("c128r", np.tile(128.0 * (15 - cc)[None], (128, 1)))
    put("c128c", np.tile(np.array([[0.0, 128.0]], np.float32), (128, 1)))
    put("c128cr", np.tile(np.array([[128.0, 0.0]], np.float32), (128, 1)))
    return P


def rope_tables(t0, n):
    pos = np.arange(t0, t0 + n)
    row = (pos // 64).astype(np.float64)
    col = (pos % 64).astype(np.float64)
    inv = 10000.0 ** (-np.arange(8, dtype=np.float64) / 8.0)
    ang = np.concatenate([row[:, None] * inv, col[:, None] * inv], -1)
    cos = np.cos(ang).T
    sin = np.sin(ang).T
    cs = np.tile(np.concatenate([cos, cos], 0), (4, 1)).astype(np.float32)
    sn = np.tile(np.concatenate([sin, sin], 0), (4, 1)).astype(np.float32)
    return np.ascontiguousarray(cs), np.ascontiguousarray(sn)


class Arena:
    def __init__(self, lo, hi, rnd=512):
        self.lo, self.hi, self.cur, self.rnd = lo, hi, lo, rnd

    def alloc(self, nbytes):
        nbytes = (nbytes + self.rnd - 1) // self.rnd * self.rnd
        o = self.cur
        assert o + nbytes <= self.hi, ("arena overflow", o, nbytes, self.hi)
        self.cur = o + nbytes
        return o

    def reset(self, to=None):
        self.cur = self.lo if to is None else to


def build_program(dbg=False, stop_after=None, skip_ctx=False, no_coll=False):
    nc = bass.Bass("TRN2", target_bir_lowering=False)
    S = Sched()
    dumps = []

    def din(name, shape, dt=F32):
        return nc.dram_tensor(name, list(shape), dt, kind="ExternalInput").ap()

    xT_d = din("xT", [D, TL])
    ctxT_d = din("ctxT", [D, TC])
    cpk_d = din("cpk", [128, CP_N])
    cs_d = din("cs", [128, TL])
    sn_d = din("sn", [128, TL])
    ident_d = din("ident", [128, 128])
    wmp_d = din("w_mod_p", [3, D, D])
    w_in_d = din("w_in", [DEPTH, D, INW])
    w_uq_d = din("w_uq", [DEPTH, 384, 768])
    w_ukv_d = din("w_ukv", [DEPTH, 256, 1024])
    pool_w_d = din("pool_w", [DEPTH, 4, 64, 64])
    w_out_d = din("w_out", [DEPTH, D, D])
    w_up_d = din("w_up", [DEPTH, D, 2 * DFF])
    w_down_d = din("w_down", [DEPTH, DFF, D])
    outT_d = nc.dram_tensor("outT", [D, TL], F32, kind="ExternalOutput").ap()
    xsp_d = nc.dram_tensor("xspill", [D, TL], F32).ap()
    expM = nc.dram_tensor("expM", [128, 48], F32).ap()
    gM = nc.dram_tensor("gM", [512, 48], F32).ap()
    expA = [nc.dram_tensor("expA%d" % l, [256, TL], BF16).ap() for l in range(DEPTH)]
    gA = [nc.dram_tensor("gA%d" % l, [4 * 256, TL], BF16).ap() for l in range(DEPTH)]
    expKn = [[nc.dram_tensor("expKn%d_%d" % (l, t), [256, TL], BF16).ap() for t in range(2)] for l in range(DEPTH)]
    gKn = [[nc.dram_tensor("gKn%d_%d" % (l, t), [1024, TL], BF16).ap() for t in range(2)] for l in range(DEPTH)]
    expV = [[nc.dram_tensor("expV%d_%d" % (l, t), [1024, 512], BF16).ap() for t in range(2)] for l in range(DEPTH)]
    gV = [[nc.dram_tensor("gV%d_%d" % (l, t), [4096, 512], BF16).ap() for t in range(2)] for l in range(DEPTH)]
    ctxKn = [nc.dram_tensor("ctxKn%d" % l, [512, TC], BF16).ap() for l in range(DEPTH)]
    ctxV = [nc.dram_tensor("ctxV%d" % l, [TC, 512], BF16).ap() for l in range(DEPTH)]
    expK = [nc.dram_tensor("expK%d" % l, [32, TL], BF16).ap() for l in range(DEPTH)]
    gK = [nc.dram_tensor("gK%d" % l, [4 * 32, TL], BF16).ap() for l in range(DEPTH)]
    ctxkv = [nc.dram_tensor("ctxkv%d" % l, [288, TC], BF16).ap() for l in range(DEPTH)]
    expB = [nc.dram_tensor("expB%d" % l, [128, 544], F32).ap() for l in range(DEPTH)]
    gB = [nc.dram_tensor("gB%d" % l, [512, 544], F32).ap() for l in range(DEPTH)]
    expC = [nc.dram_tensor("expC%d" % l, [128, 16], F32).ap() for l in range(DEPTH)]
    gC = [nc.dram_tensor("gC%d" % l, [512, 16], F32).ap() for l in range(DEPTH)]

    TOTAL = 212480
    big = nc.alloc_sbuf_tensor("big", [128, TOTAL], U8)
    ps = nc.alloc_psum_tensor("ps", [128, 7, 512], F32)
    pT = nc.alloc_psum_tensor("pT", [128, 1024], BF16)

    def V(off, shape, dt):
        n = int(np.prod(shape)) * mybir.dt.size(dt)
        assert off % 4 == 0
        ap = big[:, off:off + n].bitcast(dt)
        if len(shape) == 2:
            ap = ap.rearrange("p (a b) -> p a b", b=shape[1])
        elif len(shape) == 3:
            ap = ap.rearrange("p (a b c) -> p a b c", b=shape[1], c=shape[2])
        elif len(shape) == 4:
            ap = ap.rearrange("p (a b c d) -> p a b c d", b=shape[1], c=shape[2], d=shape[3])
        return ap

    PA = Arena(0, 50688, 64)
    XA = Arena(50688, 116224)
    YA = Arena(116224, TOTAL)

    def alloc(arena, shape, dt):
        return V(arena.alloc(int(np.prod(shape)) * mybir.dt.size(dt)), shape, dt)

    def MM(out, lhsT, rhs, start=True, stop=True, tp=None):
        if tp is None:
            S.add("pe", lambda e: e.matmul(out, lhsT, rhs, start=start, stop=stop), reads=[lhsT, rhs], writes=[out])
        else:
            S.add("pe", lambda e: e.matmul(out, lhsT, rhs, start=start, stop=stop, tile_position=tp),
                  reads=[lhsT, rhs], writes=[out])

    def TR(out, in_, ident):
        S.add("pe", lambda e: e.transpose(out, in_, ident), reads=[in_, ident], writes=[out])

    def ACT(out, in_, func, bias=None, scale=None, q="act"):
        kw = {}
        rd = [in_]
        if bias is not None:
            kw["bias"] = bias
            if not isinstance(bias, float):
                rd.append(bias)
        if scale is not None:
            kw["scale"] = scale
            if not isinstance(scale, float):
                rd.append(scale)
        S.add(q, lambda e: e.activation(out=out, in_=in_, func=func, **kw), reads=rd, writes=[out])

    def TT(q, out, a, b, op):
        S.add(q, lambda e: e.tensor_tensor(out, a, b, op), reads=[a, b], writes=[out])

    def TS1(q, out, a, s, op):
        rd = [a] + ([] if isinstance(s, float) else [s])
        S.add(q, lambda e: e.tensor_single_scalar(out, a, s, op), reads=rd, writes=[out])

    def TS2(q, out, a, s1, s2, op0, op1):
        rd = [a] + [x for x in (s1, s2) if not isinstance(x, float)]
        S.add(q, lambda e: e.tensor_scalar(out, a, s1, s2, op0, op1), reads=rd, writes=[out])

    def STT(q, out, a, sc, b, op0, op1):
        rd = [a, b] + ([] if isinstance(sc, float) else [sc])
        S.add(q, lambda e: e.scalar_tensor_tensor(out, a, sc, b, op0, op1), reads=rd, writes=[out])

    def CP(q, out, a):
        if q == "act":
            ACT(out, a, AF.Identity)
        else:
            S.add(q, lambda e: e.tensor_copy(out, a), reads=[a], writes=[out])

    def RCP(out, a):
        S.add("dve", lambda e: e.reciprocal(out, a), reads=[a], writes=[out])

    def MSET(q, ap, val):
        S.add(q, lambda e: e.memset(ap, val), reads=[], writes=[ap])

    def RSUM(out, a):
        S.add("dve", lambda e: e.reduce_sum(out, a, AX.X), reads=[a], writes=[out])

    def DMA(q, out, in_, reads, writes):
        S.add(q, lambda e: e.dma_start(out=out, in_=in_), reads=reads, writes=writes, kind="d")

    def LOAD(q, out, in_, key=None):
        DMA(q, out, in_, [key] if key else [], [out])

    def STORE(q, out, in_, key):
        DMA(q, out, in_, [in_], [key])

    def COLL(ins, outs, kin, kout):
        if no_coll:
            DMA("sp", outs[0:ins.shape[0], :], ins, [kin], [kout])
            return
        S.add("pool", lambda e: e.collective_compute(
            "AllGather", ALU.bypass, replica_groups=[[0, 1, 2, 3], [4, 5, 6, 7]],
            ins=[ins.opt()], outs=[outs.opt()]), reads=[kin], writes=[kout], kind="cc")

    def dump(name, ap, shape, dt=F32):
        if not dbg:
            return
        d = nc.dram_tensor("dbg_" + name, list(ap.shape), dt, kind="ExternalOutput").ap()
        STORE("sp", d, ap, "dbg_" + name)
        dumps.append("dbg_" + name)

    _pb = [0]

    def pbank():
        b = _pb[0]
        _pb[0] = (b + 1) % 7
        return b

    cpk = alloc(PA, [CP_N], F32)

    def C(name, *shape):
        o, n = CP_OFF[name]
        ap = cpk[:, o:o + n]
        if len(shape) == 2:
            ap = ap.rearrange("p (a b) -> p a b", b=shape[1])
        elif len(shape) == 3:
            ap = ap.rearrange("p (a b c) -> p a b c", b=shape[1], c=shape[2])
        elif len(shape) == 4:
            ap = ap.rearrange("p (a b c d) -> p a b c d", b=shape[1], c=shape[2], d=shape[3])
        return ap

    ones_bf = alloc(PA, [128], BF16)
    ident_bf = alloc(PA, [128], BF16)
    ones_f = alloc(PA, [128], F32)
    epsc = alloc(PA, [1], F32)
    zero_f = alloc(PA, [16], F32)
    modv = alloc(PA, [DEPTH, 6, KC, 2], F32)
    Amod = alloc(PA, [DEPTH, 2, KC, 2], F32)
    scv = alloc(PA, [KC, 2], F32)
    lgp = alloc(PA, [2], F32)
    lgrow = alloc(PA, [2, 128], F32)
    etmp = alloc(PA, [2, 128], F32)
    etmp2 = alloc(PA, [2, 128], F32)
    qdec = alloc(PA, [2, 128], F32)
    kdec = alloc(PA, [2, 128], F32)
    cdec = alloc(PA, [2], F32)
    gpow = alloc(PA, [2, 16], F32)
    gpowc = alloc(PA, [2, 2], F32)
    coef = alloc(PA, [2, 4], F32)
    coefc = alloc(PA, [2], F32)
    DTm = alloc(PA, [4, 128], BF16)
    dt1 = alloc(PA, [128], F32)
    dt2 = alloc(PA, [128], F32)
    sctx = alloc(PA, [2, 256], F32)
    Sin = alloc(PA, [2, 256], F32)
    Sst = alloc(PA, [2, 256], F32)
    PW = alloc(PA, [2, 128], BF16)
    cs = alloc(PA, [TL], F32)
    sn = alloc(PA, [TL], F32)
    xcT = alloc(PA, [KC, TC], F32)
    xT = alloc(XA, [KC, TL], F32)

    LOAD("sp", cpk, cpk_d)
    LOAD("sp", cs, cs_d)
    LOAD("sp", sn, sn_d)
    LOAD("pool", ident_bf, ident_d)
    MSET("pool", ones_bf, 1.0)
    MSET("pool", ones_f, 1.0)
    MSET("pool", epsc, EPS)
    MSET("pool", zero_f, 0.0)
    ACT(scv, C("scin", KC, 2), AF.Silu)
    bm = C("bm", DEPTH, 6, KC, 2)
    ymark = YA.cur
    slabs = [alloc(YA, [KC, 1024], F32) for _ in range(2)]
    expMs = alloc(YA, [48], F32)
    for i in range(3):
        slab = slabs[i % 2]
        LOAD("sp", slab, wmp_d[i].rearrange("(k p) n -> p k n", p=128))
        b = pbank()
        for oc in range(KC):
            for kc in range(KC):
                MM(ps[:, b, oc * 2:oc * 2 + 2], slab[:, kc, oc * 128:(oc + 1) * 128], scv[:, kc, :],
                   start=(kc == 0), stop=(kc == KC - 1))
        CP("dve", expMs[:, i * 16:(i + 1) * 16], ps[:, b, 0:16])
        if i == 1:
            LOAD("sp", xcT, ctxT_d.rearrange("(k p) n -> p k n", p=128))
    for tb in range(4):
        LOAD("sp", xT[:, :, tb * 512:(tb + 1) * 512],
             xT_d.rearrange("(k p) n -> p k n", p=128)[:, :, tb * 512:(tb + 1) * 512])
    STORE("sp", expM, expMs, "expM")
    COLL(expM, gM, "expM", "gM")
    mflat = modv.rearrange("p a b c d -> p (a b c d)")
    LOAD("sp", mflat.rearrange("p (r n) -> p r n", n=48), gM.rearrange("(r p) n -> p r n", p=128), "gM")
    TT("dve", mflat, mflat, bm.rearrange("p a b c d -> p (a b c d)"), ALU.add)
    YA.reset(ymark)
    n1g = C("n1g", DEPTH, KC)
    n2g = C("n2g", DEPTH, KC)
    for l in range(DEPTH):
        for wh, (gn, s6) in enumerate(((n1g, 1), (n2g, 4))):
            for m in range(2):
                STT("dve", Amod[:, l, wh, :, m], modv[:, l, s6, :, m], 1.0, gn[:, l, :], ALU.add, ALU.mult)
    dump("modv", modv, [128, DEPTH * 6 * KC * 2])
    if stop_after == "prologue":
        S.add("sp", None, reads=["outT"] + dumps, writes=[])
        S.emit(nc)
        return nc, dumps

    def mA(l, wh, kc, m):
        return Amod[:, l, wh, kc, m:m + 1]

    def mV(l, s6, kc, m):
        return modv[:, l, s6, kc, m:m + 1]

    def norm_mod(xsrc, NT, TB, l, wh, m, hdst, col0, tmpA, order=None, after=None):
        sq = [alloc(tmpA, [TB], BF16) for _ in range(2)]
        rs = [alloc(tmpA, [TB], F32) for _ in range(2)]
        tm = [alloc(tmpA, [TB], F32) for _ in range(2)]
        s_sh = 0 if wh == 0 else 3
        for ti, tb in enumerate(order if order is not None else range(NT // TB)):
            if after and ti in after:
                after[ti]()
            cols = slice(tb * TB, (tb + 1) * TB)
            b = pbank()
            for kc in range(KC):
                s = sq[kc % 2]
                if kc % 2 == 0:
                    TT("pool", s, xsrc[:, kc, cols], xsrc[:, kc, cols], ALU.mult)
                else:
                    ACT(s, xsrc[:, kc, cols], AF.Square)
                MM(ps[:, b, 0:TB], ones_bf, s, start=(kc == 0), stop=(kc == KC - 1))
            r = rs[ti % 2]
            ACT(r, ps[:, b, 0:TB], AF.Sqrt, bias=epsc[:, 0:1], scale=1.0 / D)
            RCP(r, r)
            for kc in range(KC):
                t = tm[kc % 2]
                STT("dve", t, xsrc[:, kc, cols], mA(l, wh, kc, m), r, ALU.mult, ALU.mult)
                ACT(hdst[:, kc, col0 + tb * TB: col0 + (tb + 1) * TB], t, AF.Identity, bias=mV(l, s_sh, kc, m), scale=1.0)

    def neg_log1p_exp_neg(dst, src, n):
        e = etmp[:, 0, 0:n] if n <= 128 else etmp.rearrange("p a b -> p (a b)")[:, 0:n]
        t = etmp2[:, 0, 0:n] if n <= 128 else etmp2.rearrange("p a b -> p (a b)")[:, 0:n]
        ACT(e, src, AF.Exp, scale=-1.0)
        TS2("dve", t, e, -0.2, 0.25, ALU.mult, ALU.add)
        TT("dve", t, t, e, ALU.mult)
        TS2("dve", t, t, -1.0, 1.0 / 3.0, ALU.mult, ALU.add)
        TT("dve", t, t, e, ALU.mult)
        TS2("dve", t, t, -1.0, 0.5, ALU.mult, ALU.add)
        TT("dve", t, t, e, ALU.mult)
        TS2("dve", t, t, -1.0, 1.0, ALU.mult, ALU.add)
        TT("dve", t, t, e, ALU.mult)
        TS1("dve", dst, t, -1.0, ALU.mult)

    def ret_tables(l):
        neg_log1p_exp_neg(lgp, C("dec", DEPTH, 2)[:, l, :], 2)
        neg_log1p_exp_neg(lgrow.rearrange("p a b -> p (a b)"), C("declow", DEPTH, 256)[:, l, :], 256)
        sel = C("sel")
        for d in range(2):
            lg = lgp[:, d:d + 1]
            ACT(qdec[:, d, :], C("posp1") if d == 0 else C("posm"), AF.Exp, scale=lg)
            ACT(kdec[:, d, :], lgrow[:, d, :], AF.Exp, scale=C("pcol")[:, 1:2] if d == 0 else C("pcol")[:, 0:1])
            ACT(gpow[:, d, :], C("c128") if d == 0 else C("c128r"), AF.Exp, scale=lg)
            ACT(gpowc[:, d, :], C("c128c") if d == 0 else C("c128cr"), AF.Exp, scale=lg)
            ACT(coef[:, d, :], sel[:, 8 * d:8 * d + 4], AF.Exp, scale=lg)
            TT("dve", coef[:, d, :], coef[:, d, :], sel[:, 8 * d + 4:8 * d + 8], ALU.mult)
            ACT(coefc[:, d:d + 1], sel[:, 16 + d:17 + d], AF.Exp, scale=lg)
        CP("dve", cdec[:, 0:1], gpow[:, 0, 1:2])
        CP("dve", cdec[:, 1:2], gpow[:, 1, 14:15])
        for h in range(4):
            ACT(dt1, C("dpos"), AF.Exp, scale=lgrow[:, 0, h * 32:h * 32 + 1])
            TT("dve", dt1, dt1, C("mskf"), ALU.mult)
            ACT(dt2, C("dneg"), AF.Exp, scale=lgrow[:, 1, h * 32:h * 32 + 1])
            TT("dve", dt2, dt2, C("mskb"), ALU.mult)
            TT("dve", DTm[:, h, :], dt1, dt2, ALU.add)

    def build_rot(dst, src, nb):
        d4 = dst.rearrange("p (n t s) -> p n t s", t=2, s=16)
        s4 = src.rearrange("p (n t s) -> p n t s", t=2, s=16)
        TS1("pool", d4[:, :, 0, :], s4[:, :, 1, :], -1.0, ALU.mult)
        CP("pool", d4[:, :, 1, :], s4[:, :, 0, :])

    def inproj(l, hT, col0, NT, TB, rope, o, WA, full=True):
        wst = [alloc(WA, [KC, 512], BF16) for _ in range(2)]
        wrot = alloc(WA, [KC, 256], BF16)
        sq = [alloc(WA, [TB], BF16) for _ in range(2)]
        rs = [alloc(WA, [TB], F32) for _ in range(2)]
        t1 = [alloc(WA, [TB], F32) for _ in range(2)]
        t2 = [alloc(WA, [TB], F32) for _ in range(2)]
        ntb = NT // TB
        wi = [0]

        groups = [(1152, 288), (0, 256), (256, 512)] + ([(768, 384), (1440, 256)] if full else [])
        pend = {}

        def issue(gi_):
            if gi_ < len(groups):
                c0_, n_ = groups[gi_]
                w_ = wst[gi_ % 2]
                LOAD("pool", w_[:, :, 0:n_], w_in_d[l, :, c0_:c0_ + n_].rearrange("(k p) n -> p k n", p=128))
                pend[(c0_, n_)] = w_

        issue(0)

        def loadw(c0, n):
            gi_ = groups.index((c0, n))
            w = pend[(c0, n)]
            issue(gi_ + 1)
            return w

        def chain(bank, pr, lhs_fn, rhs_fn, n):
            for kc in range(KC):
                MM(ps[pr, bank, 0:n], lhs_fn(kc), rhs_fn(kc), start=(kc == 0), stop=(kc == KC - 1))

        def hcols(tb):
            return slice(col0 + tb * TB, col0 + (tb + 1) * TB)

        rr = [0]

        def rope_evac(dst, pa, pb_, cols, prt, scale):
            i = rr[0] % 2
            rr[0] += 1
            STT("dve", t1[i][prt, :], pa, scale, cs[prt, cols], ALU.mult, ALU.mult)
            STT("dve", t2[i][prt, :], pb_, scale, sn[prt, cols], ALU.mult, ALU.mult)
            TT("pool", dst, t1[i][prt, :], t2[i][prt, :], ALU.add)

        w = loadw(1152, 288)
        if rope:
            for kc in range(KC):
                build_rot(wrot[:, kc, 0:32], w[:, kc, 256:288], 1)
        for tb in range(ntb):
            cols = slice(tb * TB, (tb + 1) * TB)
            bk = [pbank(), pbank()]
            for c in range(2):
                chain(bk[c], slice(0, 128), lambda kc: w[:, kc, c * 128:(c + 1) * 128], lambda kc: hT[:, kc, hcols(tb)], TB)
            bs = pbank()
            for c in range(2):
                ACT(sq[c], ps[:, bk[c], 0:TB], AF.Square)
                MM(ps[:, bs, 0:TB], ones_bf, sq[c], start=(c == 0), stop=(c == 1))
            r = rs[tb % 2]
            ACT(r, ps[:, bs, 0:TB], AF.Sqrt, bias=epsc[:, 0:1], scale=1.0 / 256)
            RCP(r, r)
            for c in range(2):
                STT("dve", o["ckv"][:, c, cols], ps[:, bk[c], 0:TB], C("kvng", DEPTH, 2)[:, l, c:c + 1], r, ALU.mult, ALU.mult)
            ba = pbank()
            p32 = slice(0, 32)
            chain(ba, p32, lambda kc: w[:, kc, 256:288], lambda kc: hT[:, kc, hcols(tb)], TB)
            if rope:
                bb = pbank()
                chain(bb, p32, lambda kc: wrot[:, kc, 0:32], lambda kc: hT[:, kc, hcols(tb)], TB)
                rope_evac(o["kr"][p32, cols], ps[p32, ba, 0:TB], ps[p32, bb, 0:TB], cols, p32, 1.0)
            else:
                ACT(o["kr"][p32, cols], ps[p32, ba, 0:TB], AF.Identity)
        if stop_after == "ip1":
            return
        w = loadw(0, 256)
        if rope:
            for kc in range(KC):
                build_rot(wrot[:, kc, :], w[:, kc, 0:256], 8)
        for tb in range(ntb):
            cols = slice(tb * TB, (tb + 1) * TB)
            for wh, dst, scl in ((0, o["q"], 1.0), (1, o["k"], RET_KSCALE)):
                if wh == 0 and not full:
                    continue
                ba = pbank()
                chain(ba, slice(0, 128), lambda kc: w[:, kc, wh * 128:(wh + 1) * 128], lambda kc: hT[:, kc, hcols(tb)], TB)
                if rope:
                    bb = pbank()
                    chain(bb, slice(0, 128), lambda kc: wrot[:, kc, wh * 128:(wh + 1) * 128], lambda kc: hT[:, kc, hcols(tb)], TB)
                    rope_evac(dst[:, cols], ps[:, ba, 0:TB], ps[:, bb, 0:TB], cols, slice(0, 128), scl)
                else:
                    ACT(dst[:, cols], ps[:, ba, 0:TB], AF.Identity, scale=scl)
        if stop_after == "ip2":
            return
        w = loadw(256, 512)
        for t in range(NT // 128):
            b = pbank()
            chain(b, slice(0, 128), lambda kc: hT[:, kc, col0 + t * 128: col0 + (t + 1) * 128], lambda kc: w[:, kc, 0:512], 512)
            CP("dve", o["vg"][:, t, 0:256], ps[:, b, 0:256])
            if full:
                ACT(o["vg"][:, t, 256:512], ps[:, b, 256:512], AF.Silu)
        if not full:
            return
        if stop_after == "ip3":
            return
        w = loadw(768, 384)
        for tb in range(ntb):
            cols = slice(tb * TB, (tb + 1) * TB)
            bk = [pbank(), pbank(), pbank()]
            for c in range(3):
                chain(bk[c], slice(0, 128), lambda kc: w[:, kc, c * 128:(c + 1) * 128], lambda kc: hT[:, kc, hcols(tb)], TB)
            bs = pbank()
            for c in range(3):
                ACT(sq[c % 2], ps[:, bk[c], 0:TB], AF.Square)
                MM(ps[:, bs, 0:TB], ones_bf, sq[c % 2], start=(c == 0), stop=(c == 2))
            r = rs[tb % 2]
            ACT(r, ps[:, bs, 0:TB], AF.Sqrt, bias=epsc[:, 0:1], scale=1.0 / 384)
            RCP(r, r)
            for c in range(3):
                STT("dve", o["cq"][:, c, cols], ps[:, bk[c], 0:TB], C("qng", DEPTH, 3)[:, l, c:c + 1], r, ALU.mult, ALU.mult)
        if stop_after == "ip4":
            return
        w = loadw(1440, 256)
        for tb in range(ntb):
            cols = slice(8 + tb * TB, 8 + (tb + 1) * TB)
            for c in range(2):
                b = pbank()
                chain(b, slice(0, 128), lambda kc: w[:, kc, c * 128:(c + 1) * 128], lambda kc: hT[:, kc, hcols(tb)], TB)
                ACT(o["xp"][:, c, cols], ps[:, b, 0:TB], AF.Identity)

    def ret_ktok(kT, ktok, NT):
        nch = NT // 128
        for c0 in range(0, nch, 8):
            n = min(8, nch - c0)
            for i in range(n):
                TR(pT[:, i * 128:(i + 1) * 128], kT[:, (c0 + i) * 128:(c0 + i + 1) * 128], ident_bf)
            CP("dve", ktok[:, c0:c0 + n, :], pT[:, 0:n * 128].rearrange("p (a b) -> p a b", b=128))

    class RetTmp:
        def __init__(self, RA, nch, RA2=None):
            RA2 = RA2 if RA2 is not None else RA
            self.ks = [alloc(RA, [128], BF16) for _ in range(2)]
            self.ut = [alloc(RA, [256], F32) for _ in range(2)]
            self.Sb = alloc(RA, [nch, 256], BF16)
            self.SfA = alloc(RA, [nch, 256], BF16)
            self.PTm = [alloc(RA2, [4, 128], BF16) for _ in range(2)]
            self.qs = [alloc(RA2, [2, 128], BF16) for _ in range(2)]
            self.osq = [alloc(RA2, [256], F32) for _ in range(2)]
            self.ss = [alloc(RA2, [4], F32) for _ in range(2)]
            self.ytok = [alloc(RA2, [256], BF16) for _ in range(2)]
            self.kpad = [alloc(RA2, [4, 128], BF16) for _ in range(2)]
            for kp in self.kpad:
                MSET("pool", kp.rearrange("p a b -> p (a b)"), 0.0)
            self.n = 0

    def ret_update(T, d, c, ktok, vg):
        i = T.n % 2
        T.n += 1
        TT("dve", T.ks[i], ktok[:, c, :], kdec[:, d, :], ALU.mult)
        b = pbank()
        MM(ps[:, b, 0:256], T.ks[i], vg[:, c, 0:256])
        TT("dve", T.ut[i], ps[:, b, 0:256], C("bmask"), ALU.mult)
        STT("dve", Sst[:, d, :], Sst[:, d, :], cdec[:, d:d + 1], T.ut[i], ALU.mult, ALU.add)

    def ret_pass1(T, ktok, vg, NT, dstF, dstB):
        nch = NT // 128
        MSET("dve", Sst.rearrange("p a b -> p (a b)"), 0.0)
        for i in range(nch):
            cf, cbk = i, nch - 1 - i
            CP("dve", T.SfA[:, cf, :], Sst[:, 0, :])
            ret_update(T, 0, cf, ktok, vg)
            CP("dve", T.Sb[:, cbk, :], Sst[:, 1, :])
            ret_update(T, 1, cbk, ktok, vg)
        CP("dve", dstF, Sst[:, 0, :])
        CP("dve", dstB, Sst[:, 1, :])

    def ret_pass2(T, qT, kT, ktok, vg, NT, S0, mix, col0, hooks=None, sin_fn=None):
        nch = NT // 128
        bos = {}
        bas = {}

        def H1a(c):
            i = c % 2
            ch = slice(c * 128, (c + 1) * 128)
            if hooks and c in hooks:
                hooks[c]()
            ba = pbank()
            for h in range(4):
                hp = slice(32 * h, 32 * h + 32)
                CP("act", T.kpad[i][hp, h, :], kT[hp, ch])
            for h in range(4):
                MM(ps[:, ba, h * 128:(h + 1) * 128], T.kpad[i][:, h, :], qT[:, ch])
            TT("dve", T.PTm[i], ps[:, ba, 0:512].rearrange("p (a b) -> p a b", b=128), DTm, ALU.mult)
            TT("dve", T.qs[i][:, 0, :], qT[:, ch], qdec[:, 0, :], ALU.mult)
            TT("dve", T.qs[i][:, 1, :], qT[:, ch], qdec[:, 1, :], ALU.mult)

        def H1b(c):
            i = c % 2
            bo = pbank()
            bos[c] = bo
            for h in range(4):
                hv = slice(h * 64, (h + 1) * 64)
                MM(ps[:, bo, hv], T.PTm[i][:, h, :], vg[:, c, hv], start=True, stop=False)
                MM(ps[:, bo, hv], T.qs[i][:, 0, :], T.SfA[:, c, hv], start=False, stop=False)
                MM(ps[:, bo, hv], T.qs[i][:, 1, :], T.Sb[:, c, hv], start=False, stop=True)

        def H2(c):
            i = c % 2
            bo = bos[c]
            ACT(T.osq[i], ps[:, bo, 0:256], AF.Square)
            RSUM(T.ss[i], T.osq[i].rearrange("p (a b) -> p a b", b=64))
            ACT(T.ss[i], T.ss[i], AF.Sqrt, bias=epsc[:, 0:1], scale=1.0 / 64)
            RCP(T.ss[i], T.ss[i])
            for h in range(4):
                hv = slice(h * 64, (h + 1) * 64)
                STT("dve", T.ytok[i][:, hv], ps[:, bo, hv], T.ss[i][:, h:h + 1], vg[:, c, 256 + h * 64:256 + (h + 1) * 64],
                    ALU.mult, ALU.mult)
            TR(pT[:, 0:128], T.ytok[i][:, 0:128], ident_bf)
            TR(pT[:, 128:256], T.ytok[i][:, 128:256], ident_bf)
            CP("act", mix[:, 0:2, col0 + c * 128: col0 + (c + 1) * 128], pT[:, 0:256].rearrange("p (a b) -> p a b", b=128))

        H1a(0)
        if nch > 1:
            H1a(1)
        if sin_fn is not None:
            sin_fn()
        if S0 is not None:
            for c in range(nch):
                STT("dve", T.SfA[:, c, :], S0[:, 0, :], gpow[:, 0, c:c + 1], T.SfA[:, c, :], ALU.mult, ALU.add)
                STT("dve", T.Sb[:, c, :], S0[:, 1, :], gpow[:, 1, c:c + 1], T.Sb[:, c, :], ALU.mult, ALU.add)
        H1b(0)
        for c in range(nch):
            if c + 1 < nch:
                H1b(c + 1)
            H2(c)
            if c + 2 < nch:
                H1a(c + 2)

    def pool_mix(l, xp, NT, TB, mix, col0, corr, RA, RA2=None):
        PADW = NT + 16
        A = alloc(RA, [2, PADW], F32)
        B = alloc(RA, [2, PADW], F32)
        dT = alloc(RA2 if RA2 is not None else RA, [2, NT], BF16)
        TT("dve", A[:, :, 0:PADW - 1], xp[:, :, 0:PADW - 1], xp[:, :, 1:PADW], ALU.add)
        TT("dve", B[:, :, 0:PADW - 3], A[:, :, 0:PADW - 3], A[:, :, 2:PADW - 1], ALU.add)
        TT("dve", A[:, 1, 0:PADW - 7], B[:, 1, 0:PADW - 7], B[:, 1, 4:PADW - 3], ALU.add)
        TT("dve", B[:, 1, 0:PADW - 15], A[:, 1, 0:PADW - 15], A[:, 1, 8:PADW - 7], ALU.add)
        lo, hi = slice(0, 64), slice(64, 128)
        W = [(lo, 0, A[lo, 0, 7:7 + NT]), (hi, 0, B[hi, 0, 6:6 + NT]), (lo, 1, A[lo, 1, 4:4 + NT]), (hi, 1, B[hi, 1, 0:NT])]
        invw = C("invw")
        for gi, (rows, c, Wg) in enumerate(W):
            TT("dve", Wg[:, 0:8], Wg[:, 0:8], corr[rows, c, 0:8], ALU.mult)
            TT("dve", Wg[:, NT - 8:NT], Wg[:, NT - 8:NT], corr[rows, c, 8:16], ALU.mult)
            STT("dve", dT[rows, c, :], Wg, invw[rows, c:c + 1], xp[rows, c, 8:8 + NT],
                ALU.mult, ALU.subtract)
        psc = C("psc", DEPTH, 2)
        for tb in range(NT // TB):
            cols = slice(tb * TB, (tb + 1) * TB)
            for c in range(2):
                b = pbank()
                MM(ps[:, b, 0:TB], PW[:, c, :], dT[:, c, cols])
                ACT(mix[:, 6 + c, col0 + tb * TB: col0 + (tb + 1) * TB], ps[:, b, 0:TB], AF.Identity, scale=psc[:, l, c:c + 1])

    pT32 = pT[:, :].bitcast(F32)
    misc = [ps[:, 6, :], pT32]
    _mi = [0]

    def mbank():
        b = misc[_mi[0] % 2]
        _mi[0] += 1
        return b

    def kv_produce(l, ckv, NT, TB, is_ctx, WA):
        wukv = alloc(WA, [2, 1024], BF16)
        wkn = alloc(WA, [2, 512], BF16)
        wv = alloc(WA, [2, 512], BF16)
        stg = [alloc(WA, [512], BF16) for _ in range(4)]
        LOAD("pool", wukv, w_ukv_d[l].rearrange("(k p) n -> p k n", p=128))
        for c in range(2):
            src = wukv[:, c, :].rearrange("p (h f) -> p h f", f=128)
            CP("pool", wkn[:, c, :].rearrange("p (h f) -> p h f", f=64), src[:, :, 0:64])
            CP("pool", wv[:, c, :].rearrange("p (h f) -> p h f", f=64), src[:, :, 64:128])
        n = 0
        for tb in range(NT // TB):
            cols = slice(tb * TB, (tb + 1) * TB)
            for pair in range(4):
                b = pbank()
                for c in range(2):
                    MM(ps[:, b, 0:TB], wkn[:, c, pair * 128:(pair + 1) * 128], ckv[:, c, cols], start=(c == 0), stop=(c == 1))
                sg = stg[n % 4]
                CP("dve" if n % 2 == 0 else "act", sg[:, 0:TB], ps[:, b, 0:TB])
                n += 1
                if is_ctx:
                    STORE("sp", ctxKn[l][pair * 128:(pair + 1) * 128, cols], sg[:, 0:TB], "ctxKn%d" % l)
                else:
                    STORE("sp", expKn[l][pair // 2][(pair % 2) * 128:(pair % 2) * 128 + 128, cols], sg[:, 0:TB], "expKn%d_%d" % (l, pair // 2))
        for t in range(NT // 128):
            b = pbank()
            for c in range(2):
                MM(ps[:, b, 0:512], ckv[:, c, t * 128:(t + 1) * 128], wv[:, c, :], start=(c == 0), stop=(c == 1))
            sg = stg[n % 4]
            CP("dve" if n % 2 == 0 else "act", sg, ps[:, b, 0:512])
            n += 1
            if is_ctx:
                STORE("sp", ctxV[l][t * 128:(t + 1) * 128, :], sg, "ctxV%d" % l)
            else:
                STORE("sp", expV[l][t // 8][(t % 8) * 128:(t % 8) * 128 + 128, :], sg, "expV%d_%d" % (l, t // 8))
        pend = []
        if not is_ctx:
            for t in range(2):
                pend.append(lambda t=t: COLL(expKn[l][t], gKn[l][t], "expKn%d_%d" % (l, t), "gKn%d_%d" % (l, t)))
            for t in range(2):
                pend.append(lambda t=t: COLL(expV[l][t], gV[l][t], "expV%d_%d" % (l, t), "gV%d_%d" % (l, t)))
        return pend

    def load_head_kv(l, h, KTb, Vb, with_latent):
        pair = h // 2
        t = pair // 2
        rowbase = (pair % 2) * 128 + (h % 2) * 64
        LOAD("sp", KTb[0:64, 0:TC], ctxKn[l][h * 64:(h + 1) * 64, :], "ctxKn%d" % l)
        LOAD("sp", Vb[:, 0:2, 0:64], ctxV[l][:, h * 64:(h + 1) * 64].rearrange("(t p) d -> p t d", p=128), "ctxV%d" % l)
        if with_latent:
            for r in range(4):
                LOAD("sp", KTb[0:64, TC + r * TL:TC + (r + 1) * TL], gKn[l][t][r * 256 + rowbase:r * 256 + rowbase + 64, :],
                     "gKn%d_%d" % (l, t))
                for hf in range(2):
                    LOAD("sp", Vb[:, 2 + r * 16 + hf * 8:2 + r * 16 + hf * 8 + 8, 0:64],
                         gV[l][hf][r * 1024:(r + 1) * 1024, h * 64:(h + 1) * 64].rearrange("(t p) d -> p t d", p=128),
                         "gV%d_%d" % (l, hf))

    def mla_weights(l, WA):
        wuq = alloc(WA, [3, 768], BF16)
        wuqr = alloc(WA, [3, 8, 96], BF16)
        wukv = None
        LOAD("pool", wuq, w_uq_d[l].rearrange("(k p) n -> p k n", p=128))
        MSET("pool", wuqr.rearrange("p a b c -> p (a b c)"), 0.0)
        for c in range(3):
            src = wuq[:, c, :].rearrange("p (h f) -> p h f", f=96)[:, :, 64:96].rearrange("p h (t s) -> p h t s", s=16)
            dst = wuqr[:, c, :, 64:96].rearrange("p h (t s) -> p h t s", s=16)
            TS1("pool", dst[:, :, 0, :], src[:, :, 1, :], -1.0, ALU.mult)
            CP("pool", dst[:, :, 1, :], src[:, :, 0, :])
        return wuq, wuqr, wukv

    def kv_blocks(l, with_latent):
        blks = [(ctxkv[l][0:256, :], "ctxkv%d" % l, TC, 0)]
        if with_latent:
            for r in range(4):
                for jj in range(4):
                    blks.append((gA[l][r * 256:r * 256 + 256, jj * 512:(jj + 1) * 512], "gA%d" % l, 512, TC + r * TL + jj * 512))
        return blks

    def kv_head_steps(l, h, KTb, Vb, wukv, blks, stg):
        for bi, (src, key, n, k0) in enumerate(blks):
            s = stg[bi % 2]
            LOAD("sp", s[:, :, 0:n], src.rearrange("(c p) n -> p c n", p=128), key)
            mb = mbank()
            for c in range(2):
                MM(mb[0:64, 0:n], wukv[:, c, h * 128:h * 128 + 64], s[:, c, 0:n], start=(c == 0), stop=(c == 1))
            CP("dve", KTb[0:64, k0:k0 + n], mb[0:64, 0:n])
            mb = mbank()
            nt = n // 128
            for t in range(nt):
                for c in range(2):
                    MM(mb[:, t * 64:(t + 1) * 64], s[:, c, t * 128:(t + 1) * 128], wukv[:, c, h * 128 + 64:h * 128 + 128],
                       start=(c == 0), stop=(c == 1))
            CP("dve", Vb[:, k0 // 128:k0 // 128 + nt, 0:64], mb[:, 0:nt * 64].rearrange("p (a b) -> p a b", b=64))
            yield

    def kr_rows(l, KTb, with_latent):
        LOAD("sp", KTb[64:96, 0:TC], ctxkv[l][256:288, :], "ctxkv%d" % l)
        if with_latent:
            for r in range(4):
                LOAD("sp", KTb[64:96, TC + r * TL:TC + (r + 1) * TL], gK[l][r * 32:r * 32 + 32, :], "gK%d" % l)

    def q_head_steps(l, h, QTb, cq, wuq, wuqr, NT, TB, rope, tq):
        for tb in range(NT // TB):
            cols = slice(tb * TB, (tb + 1) * TB)
            ma = mbank()
            for c in range(3):
                MM(ma[0:96, 0:TB], wuq[:, c, h * 96:(h + 1) * 96], cq[:, c, cols], start=(c == 0), stop=(c == 2))
            CP("dve", QTb[0:64, cols], ma[0:64, 0:TB])
            pr = slice(64, 96)
            if rope:
                mb = mbank()
                for c in range(3):
                    MM(mb[0:96, 0:TB], wuqr[:, c, h, :], cq[:, c, cols], start=(c == 0), stop=(c == 2))
                TT("dve", tq[0][pr, 0:TB], ma[pr, 0:TB], cs[pr, cols], ALU.mult)
                TT("dve", tq[1][pr, 0:TB], mb[pr, 0:TB], sn[pr, cols], ALU.mult)
                TT("pool", QTb[pr, cols], tq[0][pr, 0:TB], tq[1][pr, 0:TB], ALU.add)
            else:
                CP("dve", QTb[pr, cols], ma[pr, 0:TB])
            yield

    def attend(h, QTb, KTb, Vb, nkb, NT, mix, col0, PTs, OTs, dn, bg, st):
        QCW = min(1024, NT)
        Wd = min(512, QCW)
        nh = QCW // Wd
        po = (h % 2) * 64
        pr = slice(po, po + 64)

        def qk(qc, kb):
            sb = kb % 2
            for hf in range(nh):
                q0 = qc * QCW + hf * Wd
                MM(ps[:, 2 * sb + hf, 0:Wd], KTb[0:96, kb * 128:(kb + 1) * 128], QTb[0:96, q0:q0 + Wd])

        for qc in range(NT // QCW):
            for i in range(nkb + 2):
                if i == 3 and st.get("pend") is not None:
                    st["pend"]()
                    st["pend"] = None
                if i < nkb:
                    qk(qc, i)
                    sb = i % 2
                    P = PTs[i % 3]
                    ACT(P[:, 0:QCW].rearrange("p (a b) -> p a b", b=Wd), ps[:, 2 * sb:2 * sb + nh, 0:Wd], AF.Exp, scale=MLA_SCALE)
                if i >= 2:
                    kb = i - 2
                    P = PTs[kb % 3]
                    for hf in range(nh):
                        MM(ps[0:65, 4 + hf, 0:Wd], Vb[:, kb, 0:65], P[:, hf * Wd:(hf + 1) * Wd], start=(kb == 0), stop=(kb == nkb - 1))
                    if bg is not None and kb % 5 == 4:
                        next(bg, None)
            if st.get("pend") is not None:
                st["pend"]()
                st["pend"] = None
            CP("dve", OTs[pr, 0:QCW].rearrange("p (a b) -> p a b", b=Wd), ps[0:64, 4:4 + nh, 0:Wd])
            CP("dve", dn[0:1, 0:QCW].rearrange("p (a b) -> p a b", b=Wd), ps[64:65, 4:4 + nh, 0:Wd])
            RCP(dn[0:1, 0:QCW], dn[0:1, 0:QCW])

            def finish(qc=qc):
                for hf in range(nh):
                    mb = mbank()
                    MM(mb[:, 0:Wd], ones_f[0:1, 0:128], dn[0:1, hf * Wd:(hf + 1) * Wd])
                    q0 = col0 + qc * QCW + hf * Wd
                    TT("dve", mix[pr, 2 + h // 2, q0:q0 + Wd], OTs[pr, hf * Wd:(hf + 1) * Wd], mb[pr, 0:Wd], ALU.mult)

            st["pend"] = finish

    def mla(l, cq, NT, TB, rope, with_latent, mix, col0, WA, KA):
        wuq, wuqr, wukv = mla_weights(l, WA)
        nkeys = NKEY if with_latent else TC
        nkb = nkeys // 128
        QT = [alloc(WA, [NT], BF16) for _ in range(2)]
        tq = [alloc(WA, [TB], F32) for _ in range(2)]
        OTs = alloc(WA, [min(1024, NT)], F32)
        dn = alloc(WA, [min(1024, NT)], F32)
        KT = [alloc(KA, [nkeys], BF16) for _ in range(2)]
        Vb = [alloc(KA, [nkb, 80], BF16) for _ in range(2)]
        PTs = [alloc(KA, [min(1024, NT)], BF16) for _ in range(3)]
        for i in range(2):
            kr_rows(l, KT[i], with_latent)
            MSET("pool", Vb[i][:, :, 64:65], 1.0)

        def prod(h):
            load_head_kv(l, h, KT[h % 2], Vb[h % 2], with_latent)
            for _ in q_head_steps(l, h, QT[h % 2], cq, wuq, wuqr, NT, TB, rope, tq):
                yield

        for _ in prod(0):
            pass
        st = {"pend": None}
        for h in range(8):
            bg = prod(h + 1) if h + 1 < 8 else None
            attend(h, QT[h % 2], KT[h % 2], Vb[h % 2], nkb, NT, mix, col0, PTs, OTs, dn, bg, st)
            if bg is not None:
                for _ in bg:
                    pass
        if st["pend"] is not None:
            st["pend"]()

    def out_proj(wout, mix, col0, NT, TB, xdst, l, m, order=None):
        for tb in (order if order is not None else range(NT // TB)):
            for oc in range(KC):
                b = pbank()
                for kc in range(KC):
                    MM(ps[:, b, 0:TB], wout[:, kc, oc * 128:(oc + 1) * 128], mix[:, kc, col0 + tb * TB: col0 + (tb + 1) * TB],
                       start=(kc == 0), stop=(kc == KC - 1))
                xs = xdst[:, oc, tb * TB:(tb + 1) * TB]
                STT("dve", xs, ps[:, b, 0:TB], mV(l, 2, oc, m), xs, ALU.mult, ALU.add)

    def ffn(l, h2T, NT, PASS, xdst, m, WA, depth=2):
        SBW = min(512, PASS)
        nsb = PASS // SBW
        GH = NG // 2
        actT = alloc(WA, [GH, PASS], BF16)
        wab = [alloc(WA, [2, KC, 128], BF16) for _ in range(depth)]
        wd = [alloc(WA, [GH, 128], BF16) for _ in range(2)]
        acc = [alloc(WA, [2, SBW], F32) for _ in range(2)]
        sact = [alloc(WA, [SBW], F32) for _ in range(2)]
        cw = C("convw", DEPTH, 3, 44)
        cb = C("convb", DEPTH, 44)
        n = [0, 0, 0, 0]
        wab.append(alloc(WA, [2, KC, 128], BF16))
        items = [(p, half, gi) for p in range(NT // PASS) for half in range(2) for gi in range(GH)]

        def load_w(i):
            p, half, gi = items[i]
            g = half * GH + gi
            w = wab[i % (depth + 1)]
            LOAD("pool", w[:, 0, :, :], w_up_d[l, :, g * 128:(g + 1) * 128].rearrange("(k p) n -> p k n", p=128))
            LOAD("pool", w[:, 1, :, :], w_up_d[l, :, DFF + g * 128:DFF + (g + 1) * 128].rearrange("(k p) n -> p k n", p=128))

        def load_wd(half, oc):
            wdd = wd[oc % 2]
            LOAD("pool", wdd, w_down_d[l, half * GH * 128:(half + 1) * GH * 128, oc * 128:(oc + 1) * 128]
                 .rearrange("(g p) n -> p g n", p=128))

        tail = [None]
        for j0 in range(min(depth, len(items))):
            load_w(j0)
        for i, (p, half, gi) in enumerate(items):
            g = half * GH + gi
            w = wab[i % (depth + 1)]
            if i + depth < len(items):
                load_w(i + depth)
            if gi == GH - 1:
                load_wd(half, 0)
            for sb in range(nsb):
                t0 = p * PASS + sb * SBW
                a = acc[n[1] % 2]
                n[1] += 1
                for ab in range(2):
                    ch = g + ab * NG
                    bmn = n[2] % 4
                    n[2] += 1
                    hc = (n[3] % 64) * 4
                    hbk = ps[:, 4, :] if n[3] % 2 == 0 else pT32
                    n[3] += 1
                    for kc in range(KC):
                        MM(ps[:, bmn, 0:SBW], w[:, ab, kc, :], h2T[:, kc, 1 + t0:1 + t0 + SBW],
                           start=(kc == 0), stop=(kc == KC - 1))
                    for kc in range(KC):
                        MM(hbk[:, hc:hc + 2], w[:, ab, kc, :], h2T[:, kc, t0:t0 + SBW + 2:SBW + 1],
                           start=(kc == 0), stop=(kc == KC - 1))
                    main = ps[:, bmn, 0:SBW]
                    aa = a[:, ab, :]
                    ACT(aa, main, AF.Identity, bias=cb[:, l, ch:ch + 1], scale=cw[:, l, 1, ch:ch + 1])
                    STT("dve", aa[:, 1:SBW], main[:, 0:SBW - 1], cw[:, l, 0, ch:ch + 1], aa[:, 1:SBW], ALU.mult, ALU.add)
                    STT("dve", aa[:, 0:1], hbk[:, hc:hc + 1], cw[:, l, 0, ch:ch + 1], aa[:, 0:1], ALU.mult, ALU.add)
                    STT("dve", aa[:, 0:SBW - 1], main[:, 1:SBW], cw[:, l, 2, ch:ch + 1], aa[:, 0:SBW - 1], ALU.mult, ALU.add)
                    STT("dve", aa[:, SBW - 1:SBW], hbk[:, hc + 1:hc + 2], cw[:, l, 2, ch:ch + 1], aa[:, SBW - 1:SBW],
                        ALU.mult, ALU.add)
                if tail[0] is not None:
                    tail[0]()

                def mk_tail(a=a, sa=sact[n[1] % 2], gi=gi, sb=sb):
                    ACT(sa, a[:, 0, :], AF.Silu)
                    TT("pool", actT[:, gi, sb * SBW:(sb + 1) * SBW], sa, a[:, 1, :], ALU.mult)

                tail[0] = mk_tail
            if gi == GH - 1:
                if tail[0] is not None:
                    tail[0]()
                    tail[0] = None
                for oc in range(KC):
                    wdd = wd[oc % 2]
                    if oc + 1 < KC:
                        load_wd(half, oc + 1)
                    for sb in range(nsb):
                        t0 = p * PASS + sb * SBW
                        b = 5 + (oc * nsb + sb) % 2
                        for gj in range(GH):
                            MM(ps[:, b, 0:SBW], wdd[:, gj, :], actT[:, gj, sb * SBW:(sb + 1) * SBW],
                               start=(gj == 0), stop=(gj == GH - 1))
                        xs = xdst[:, oc, t0:t0 + SBW]
                        STT("dve", xs, ps[:, b, 0:SBW], mV(l, 5, oc, m), xs, ALU.mult, ALU.add)

    def load_layer_consts(l):
        ret_tables(l)
        MSET("pool", PW.rearrange("p a b -> p (a b)"), 0.0)
        for c in range(2):
            LOAD("pool", PW[0:64, c, 0:64], pool_w_d[l, 2 * c])
            LOAD("pool", PW[64:128, c, 64:128], pool_w_d[l, 2 * c + 1])

    def load_wout(l, WA):
        wout = alloc(WA, [KC, D], BF16)
        LOAD("pool", wout, w_out_d[l].rearrange("(k p) n -> p k n", p=128))
        return wout

    def ctx_layer(l):
        full = (l < DEPTH - 1) and stop_after not in ("retpool", "mla", "outproj")
        YA.reset()
        NT, TB = TC, TC
        hT = alloc(YA, [KC, NT + 32], BF16)
        o = {"cq": alloc(YA, [3, NT], BF16), "ckv": alloc(YA, [2, NT], BF16), "kr": alloc(YA, [NT], BF16),
             "q": alloc(YA, [NT], BF16), "k": alloc(YA, [NT], BF16), "vg": alloc(YA, [NT // 128, 512], BF16),
             "xp": alloc(YA, [2, NT + 16], F32)}
        ktok = alloc(YA, [NT // 128, 128], BF16)
        ym = YA.cur
        norm_mod(xcT, NT, TB, l, 0, 1, hT, 16, YA)
        YA.reset(ym)
        inproj(l, hT, 16, NT, TB, False, o, YA, full=full)
        YA.reset(ym)
        STORE("sp", ctxkv[l][256:288, :], o["kr"][0:32, :], "ctxkv%d" % l)
        kv_produce(l, o["ckv"], NT, TB, True, YA)
        YA.reset(ym)
        ret_ktok(o["k"], ktok, NT)
        T = RetTmp(YA, NT // 128)
        ret_pass1(T, ktok, o["vg"], NT, sctx[:, 0, :], sctx[:, 1, :])
        if not full:
            return
        ret_pass2(T, o["q"], o["k"], ktok, o["vg"], NT, None, hT, 16)
        for c in range(2):
            MSET("pool", o["xp"][:, c, 0:8], 0.0)
            MSET("pool", o["xp"][:, c, 8 + NT:16 + NT], 0.0)
        pool_mix(l, o["xp"], NT, TB, hT, 16, C("poolcc", 2, 16), YA)
        wout = load_wout(l, YA)
        mla(l, o["cq"], NT, TB, False, False, hT, 16, YA, YA)
        out_proj(wout, hT, 16, NT, TB, xcT, l, 1)
        dump("xc_mid%d" % l, xcT, [128, KC * TC])
        YA.reset()
        h2 = alloc(YA, [KC, NT + 2], BF16)
        MSET("pool", h2[:, :, 0:1], 0.0)
        MSET("pool", h2[:, :, NT + 1:NT + 2], 0.0)
        norm_mod(xcT, NT, TB, l, 1, 1, h2, 1, YA)
        ffn(l, h2, NT, NT, xcT, 1, YA, depth=5)
        dump("xc_out%d" % l, xcT, [128, KC * TC])

    def latent_layer(l):
        NT, TB = TL, 512
        YA.reset()
        B32 = alloc(YA, [KC, NT + 32], BF16)
        cq = alloc(YA, [3, NT], BF16)
        ym0 = YA.cur
        ckv = alloc(YA, [2, NT], BF16)
        kr = alloc(YA, [NT], BF16)
        ym = YA.cur
        norm_mod(xT, NT, TB, l, 0, 0, B32, 16, YA)
        YA.reset(ym)
        if stop_after == "norm":
            dump("hT", B32, [128, KC * (NT + 32)], BF16)
            return False
        if l > 0:
            STORE("sp", xsp_d.rearrange("(k p) n -> p k n", p=128), xT, "xsp")
        XA.reset()
        o = {"cq": cq, "ckv": ckv, "kr": kr, "q": alloc(XA, [NT], BF16), "k": alloc(XA, [NT], BF16),
             "vg": alloc(XA, [NT // 128, 512], BF16), "xp": alloc(XA, [2, NT + 16], F32)}
        ktok = alloc(XA, [NT // 128, 128], BF16)
        Lall = alloc(XA, [4, 544], F32)
        expBs = alloc(XA, [544], F32)
        xm = XA.cur
        inproj(l, B32, 16, NT, TB, True, o, YA)
        YA.reset(ym)
        if dbg and l == 0:
            dump("hT", B32, [128, KC * (NT + 32)], BF16)
            dump("qT", o["q"], [128, NT], BF16)
            dump("kT", o["k"], [128, NT], BF16)
            dump("vg", o["vg"], [128, (NT // 128) * 512], BF16)
            dump("cq", cq, [128, 3 * NT], BF16)
            dump("ckv", ckv, [128, 2 * NT], BF16)
            dump("kr", kr, [128, NT], BF16)
            dump("xp", o["xp"], [128, 2 * (NT + 16)])
        if stop_after in ("inproj", "ip1", "ip2", "ip3", "ip4"):
            return False
        STORE("sp", expK[l][:, :], kr[0:32, :], "expK%d" % l)
        COLL(expK[l], gK[l], "expK%d" % l, "gK%d" % l)
        kvc = kv_produce(l, ckv, NT, TB, False, YA)
        ymk = YA.cur
        ret_ktok(o["k"], ktok, NT)
        T = RetTmp(YA, NT // 128, Arena(ym, ymk))
        ret_pass1(T, ktok, o["vg"], NT, expBs[:, 32:288], expBs[:, 288:544])
        for c in range(2):
            CP("act", expBs[:, c * 16:c * 16 + 8], o["xp"][:, c, 8:16])
            CP("act", expBs[:, c * 16 + 8:c * 16 + 16], o["xp"][:, c, NT:NT + 8])
        STORE("sp", expB[l], expBs, "expB%d" % l)
        COLL(expB[l], gB[l], "expB%d" % l, "gB%d" % l)
        sel = C("sel")

        def sin_fn():
            LOAD("sp", Lall, gB[l].rearrange("(r p) n -> p r n", p=128), "gB%d" % l)
            for d in range(2):
                TS1("dve", Sin[:, d, :], sctx[:, d, :], coefc[:, d:d + 1], ALU.mult)
                for r in range(4):
                    STT("dve", Sin[:, d, :], Lall[:, r, 32 + 256 * d:288 + 256 * d], coef[:, d, r:r + 1], Sin[:, d, :], ALU.mult, ALU.add)
            for c in range(2):
                for side, (dst, so, scol) in enumerate(((o["xp"][:, c, 0:8], 8, 20), (o["xp"][:, c, 8 + NT:16 + NT], 0, 24))):
                    TS1("dve", dst, Lall[:, 0, c * 16 + so:c * 16 + so + 8], sel[:, scol:scol + 1], ALU.mult)
                    for r in range(1, 4):
                        STT("dve", dst, Lall[:, r, c * 16 + so:c * 16 + so + 8], sel[:, scol + r:scol + r + 1], dst, ALU.mult, ALU.add)

        ret_pass2(T, o["q"], o["k"], ktok, o["vg"], NT, Sin, B32, 16, hooks={2: kvc[0], 6: kvc[1], 10: kvc[2], 14: kvc[3]}, sin_fn=sin_fn)
        YA.reset(ym)
        XA.reset(xm)
        pool_mix(l, o["xp"], NT, TB, B32, 16, C("poolc", 2, 16), YA, XA)
        if dbg and l == 0:
            dump("mix_rp", B32, [128, KC * (NT + 32)], BF16)
        if stop_after == "retpool":
            return False
        YA.reset(ym0)
        XA.reset()
        wout = load_wout(l, YA)
        mla(l, cq, NT, TB, True, True, B32, 16, YA, XA)
        if dbg and l == 0:
            dump("mix", B32, [128, KC * (NT + 32)], BF16)
        if stop_after == "mla":
            return False
        XA.reset()
        src = xT_d if l == 0 else xsp_d
        for tb in range(4):
            LOAD("sp", xT[:, :, tb * 512:(tb + 1) * 512],
                 src.rearrange("(k p) n -> p k n", p=128)[:, :, tb * 512:(tb + 1) * 512], None if l == 0 else "xsp")
        out_proj(wout, B32, 16, NT, TB, xT, l, 0, order=[3, 0, 1, 2])
        if dbg and l == 0:
            dump("x_mid", xT, [128, KC * NT])
        if stop_after == "outproj":
            return False
        CA = Arena(YA.lo + 33280, ym0)
        YA.reset(ym0 + 16384)
        h2 = alloc(YA, [KC, NT + 2], BF16)
        stC = alloc(CA, [KC, 2], F32)
        gCs = alloc(CA, [4, 16], F32)
        hal = alloc(CA, [16], F32)
        def export_halo():
            CP("act", stC[:, :, 0], h2[:, :, 1])
            CP("act", stC[:, :, 1], h2[:, :, NT])
            STORE("sp", expC[l], stC.rearrange("p a b -> p (a b)"), "expC%d" % l)
            COLL(expC[l], gC[l], "expC%d" % l, "gC%d" % l)

        norm_mod(xT, NT, TB, l, 1, 0, h2, 1, CA, order=[3, 0, 1, 2], after={2: export_halo})
        FA = Arena(YA.lo, ym0 + 16384)
        LOAD("sp", gCs, gC[l].rearrange("(r p) n -> p r n", p=128), "gC%d" % l)
        g3 = gCs.rearrange("p r (k s) -> p r k s", s=2)
        h3 = hal.rearrange("p (k s) -> p k s", s=2)
        for side, (srcs, scol) in enumerate(((1, 20), (0, 24))):
            dst = h3[:, :, side]
            TS1("dve", dst, g3[:, 0, :, srcs], sel[:, scol:scol + 1], ALU.mult)
            for r in range(1, 4):
                STT("dve", dst, g3[:, r, :, srcs], sel[:, scol + r:scol + r + 1], dst, ALU.mult, ALU.add)
        CP("act", h2[:, :, 0], h3[:, :, 0])
        CP("act", h2[:, :, NT + 1], h3[:, :, 1])
        ffn(l, h2, NT, 1024, xT, 0, FA)
        if dbg:
            dump("x_out%d" % l, xT, [128, KC * NT])
        return True

    def final_norm():
        YA.reset()
        NT, TB = TL, 512
        sq = [alloc(YA, [TB], BF16) for _ in range(2)]
        rs = [alloc(YA, [TB], F32) for _ in range(2)]
        ob = [alloc(YA, [KC, TB], F32) for _ in range(2)]
        fng = C("fng")
        for tb in range(NT // TB):
            cols = slice(tb * TB, (tb + 1) * TB)
            b = pbank()
            for kc in range(KC):
                s = sq[kc % 2]
                TT("pool", s, xT[:, kc, cols], xT[:, kc, cols], ALU.mult)
                MM(ps[:, b, 0:TB], ones_bf, s, start=(kc == 0), stop=(kc == KC - 1))
            r = rs[tb % 2]
            ACT(r, ps[:, b, 0:TB], AF.Sqrt, bias=epsc[:, 0:1], scale=1.0 / D)
            RCP(r, r)
            for kc in range(KC):
                STT("dve", ob[tb % 2][:, kc, :], xT[:, kc, cols], fng[:, kc:kc + 1], r, ALU.mult, ALU.mult)
            STORE("sp", outT_d.rearrange("(k p) n -> p k n", p=128)[:, :, cols], ob[tb % 2], "outT")

    ok = True
    for l in range(DEPTH):
        load_layer_consts(l)
        if not skip_ctx:
            ctx_layer(l)
        ok = latent_layer(l)
        if not ok:
            break
    if ok:
        final_norm()
    keys_out = ["outT"] + dumps
    S.add("sp", None, reads=keys_out, writes=[])
    S.emit(nc)
    return nc, dumps


_PROG = {}


def _get_prog(dbg=False, stop_after=None):
    key = (dbg, stop_after)
    if key not in _PROG:
        _PROG[key] = build_program(dbg, stop_after)
    return _PROG[key]


def make_in_maps(inp):
    inp = {k: np.asarray(v, np.float32) for k, v in inp.items()}
    shared = {k: np.ascontiguousarray(inp[k]) for k in
              ("w_in", "w_uq", "w_ukv", "pool_w", "w_out", "w_up", "w_down")}
    wm = inp["w_mod"].reshape(DEPTH, D, 6, D).transpose(0, 2, 1, 3).reshape(DEPTH * 6, D, D)
    ident = np.eye(128, dtype=np.float32)
    maps = []
    for core in range(NCORES):
        b, j = core // 4, core % 4
        cs, sn = rope_tables(j * TL, TL)
        m = dict(shared)
        m["xT"] = np.ascontiguousarray(inp["x"][b, j * TL:(j + 1) * TL, :].T)
        m["ctxT"] = np.ascontiguousarray(inp["ctx"][b].T)
        m["cpk"] = build_cpk(core, inp)
        m["w_mod_p"] = np.ascontiguousarray(wm[3 * j:3 * j + 3])
        m["cs"] = cs
        m["sn"] = sn
        m["ident"] = ident
        maps.append(m)
    return maps


def kernel(**inputs):
    nc, _ = _get_prog()
    maps = make_in_maps(inputs)
    res = run_bass_kernel_spmd(nc, maps, core_ids=list(range(NCORES)))
    out = np.empty((2, 4 * TL, D), np.float32)
    for core in range(NCORES):
        b, j = core // 4, core % 4
        out[b, j * TL:(j + 1) * TL, :] = np.asarray(res.results[core]["outT"]).T
    return out
```

```python
import numpy as np
from contextlib import ExitStack
import concourse.bass as bass
import concourse.mybir as mybir
from concourse.bass_utils import run_bass_kernel_spmd

F32 = mybir.dt.float32
BF16 = mybir.dt.bfloat16
U8 = mybir.dt.uint8
AF = mybir.ActivationFunctionType
ALU = mybir.AluOpType
AX = mybir.AxisListType

NCORES = 8
D = 1024
KC = 8
TL = 2048
TC = 256
DEPTH = 2
DFF = 2816
NG = 22
INW = 1696
EPS = 1e-6
MLA_SCALE = 96 ** -0.5
RET_KSCALE = 32 ** -0.5
NKEY = TC + 4 * TL


class _Op:
    __slots__ = ("q", "fn", "kind", "raw", "oth", "waits", "sig", "dslot", "dval", "dprev", "idx")


QS = ("pe", "act", "dve", "pool", "sp")
NDS = 8


class Sched:
    def __init__(self):
        self.ops = []
        self.lw = {}
        self.rdc = {}
        self.rdd = {}

    @staticmethod
    def keys(items):
        out = []
        for x in items:
            if x is None:
                continue
            if isinstance(x, (str, tuple)):
                out.append(x)
                continue
            sp = str(x.space)
            dims = x.ap
            esz = mybir.dt.size(x.dtype)
            pstr = dims[0][0]
            off = x.offset % pstr if pstr > 0 else x.offset
            ext = 1 + sum((c - 1) * st for st, c in dims[1:])
            lo = off * esz
            hi = (off + ext) * esz
            blk = 2048 if "PSUM" in sp else 512
            nm = x.name
            out.extend((nm, b) for b in range(lo // blk, (hi - 1) // blk + 1))
        return out

    def add(self, q, fn, reads=(), writes=(), kind="c"):
        op = _Op()
        op.q = q
        op.fn = fn
        op.kind = kind
        op.idx = len(self.ops)
        op.sig = 0
        rk = self.keys(reads)
        wk = self.keys(writes)
        psr = [k for k in rk if isinstance(k, tuple) and k[0] in ("ps", "pT")]
        if psr:
            wk = wk + [k for k in psr if k not in wk]
        raw = set()
        oth = set()
        for k in rk:
            w = self.lw.get(k)
            if w is not None:
                raw.add(w)
        for k in wk:
            w = self.lw.get(k)
            if w is not None:
                oth.add(w)
            d = self.rdc.get(k)
            if d:
                oth.update(d.values())
            l = self.rdd.get(k)
            if l:
                oth.update(l)
        for k in rk:
            if kind == "c":
                self.rdc.setdefault(k, {})[q] = op.idx
            else:
                l = self.rdd.setdefault(k, [])
                l.append(op.idx)
                if len(l) > 24:
                    del l[0]
        for k in wk:
            self.lw[k] = op.idx
            self.rdc[k] = {}
            self.rdd[k] = []
        raw.discard(op.idx)
        oth.discard(op.idx)
        oth -= raw
        op.raw = raw
        op.oth = oth
        self.ops.append(op)
        return op

    def _needs(self, op, d):
        dop = self.ops[d]
        if dop.kind != "c":
            return True
        if dop.q == op.q and op.kind == "c" and op.q == "pe":
            return False
        return True

    def finalize(self):
        need = set()
        for op in self.ops:
            for d in op.raw | op.oth:
                if self.ops[d].kind == "c" and self._needs(op, d):
                    need.add(d)
        cnt = {q: 0 for q in QS}
        dcnt = {q: 0 for q in QS}
        ccn = 0
        for op in self.ops:
            if op.kind == "c":
                if op.idx in need:
                    cnt[op.q] += 1
                    op.sig = cnt[op.q]
            elif op.kind == "d":
                i = dcnt[op.q]
                dcnt[op.q] += 1
                op.dslot = i % NDS
                op.dval = 16 * (i // NDS + 1)
                op.dprev = 16 * (i // NDS)
            else:
                ccn += 1
                op.dval = ccn
        waited = {q: {} for q in QS}
        for op in self.ops:
            w = {}
            wq = waited[op.q]
            for d in op.raw | op.oth:
                if not self._needs(op, d):
                    continue
                dop = self.ops[d]
                if dop.kind == "c":
                    key, val = ("c", dop.q), dop.sig
                elif dop.kind == "d":
                    key, val = ("d", dop.q, dop.dslot), dop.dval
                else:
                    key, val = ("cc",), dop.dval
                if wq.get(key, 0) < val and w.get(key, 0) < val:
                    w[key] = val
            if op.kind == "d" and op.dprev:
                key = ("d", op.q, op.dslot)
                if wq.get(key, 0) < op.dprev and w.get(key, 0) < op.dprev:
                    w[key] = op.dprev
            wq.update(w)
            op.waits = list(w.items())
        self.nsig = cnt

    def emit(self, nc):
        self.finalize()
        with ExitStack() as st:
            sems = {}
            for q in ("pe", "act", "dve", "pool"):
                sems[("c", q)] = st.enter_context(nc.semaphore("c_" + q))
            for q in ("sp", "pool", "act"):
                for i in range(NDS):
                    sems[("d", q, i)] = st.enter_context(nc.semaphore("d_%s_%d" % (q, i)))
            sems[("cc",)] = st.enter_context(nc.semaphore("ccs"))
            block = st.enter_context(nc.Block())
            byq = {q: [op for op in self.ops if op.q == q] for q in QS}

            def run(q, e):
                for op in byq[q]:
                    for key, val in op.waits:
                        e.wait_ge(sems[key], val)
                    if op.fn is None:
                        continue
                    ins = op.fn(e)
                    if op.kind == "c":
                        if op.sig:
                            ins.then_inc(sems[("c", q)], 1)
                    elif op.kind == "d":
                        ins.then_inc(sems[("d", q, op.dslot)], 16)
                    else:
                        ins.then_inc(sems[("cc",)])

            @block.tensor
            def _(e):
                run("pe", e)

            @block.scalar
            def _(e):
                run("act", e)

            @block.vector
            def _(e):
                run("dve", e)

            @block.gpsimd
            def _(e):
                run("pool", e)

            @block.sync
            def _(e):
                run("sp", e)


def _layout():
    off = {}
    cur = 0

    def f(name, n):
        nonlocal cur
        off[name] = (cur, n)
        cur += n

    f("scin", 16)
    f("bm", 192)
    f("n1g", 16)
    f("n2g", 16)
    f("fng", 8)
    f("qng", 6)
    f("kvng", 4)
    f("psc", 4)
    f("convw", 264)
    f("convb", 88)
    f("dec", 4)
    f("sel", 32)
    f("invw", 2)
    f("poolc", 32)
    f("poolcc", 32)
    f("pcol", 2)
    f("declow", 512)
    f("dpos", 128)
    f("dneg", 128)
    f("mskf", 128)
    f("mskb", 128)
    f("bmask", 256)
    f("posp1", 128)
    f("posm", 128)
    f("c128", 16)
    f("c128r", 16)
    f("c128c", 2)
    f("c128cr", 2)
    return off, cur


CP_OFF, CP_N = _layout()


def _np_layout_vec(v):
    v = np.asarray(v, np.float32)
    lead = v.shape[:-1]
    n = v.shape[-1] // 128
    r = v.reshape(lead + (n, 128))
    r = np.moveaxis(r, -1, 0)
    return np.ascontiguousarray(r)


def build_cpk(core, inp):
    b, j = core // 4, core % 4
    P = np.zeros((128, CP_N), np.float32)

    def put(name, arr):
        o, n = CP_OFF[name]
        arr = np.asarray(arr, np.float32).reshape(128, -1)
        assert arr.shape[1] == n, (name, arr.shape, n)
        P[:, o:o + n] = arr

    sc = np.stack([_np_layout_vec(inp["c"][b]), _np_layout_vec(inp["c_ctx"])], -1)
    put("scin", sc)
    bm = _np_layout_vec(inp["b_mod"].reshape(DEPTH, 6, D))
    put("bm", np.repeat(bm[..., None], 2, -1))
    put("n1g", _np_layout_vec(inp["norm1_g"]))
    put("n2g", _np_layout_vec(inp["norm2_g"]))
    put("fng", _np_layout_vec(inp["final_norm_g"]))
    put("qng", _np_layout_vec(inp["mla_q_norm_g"]))
    put("kvng", _np_layout_vec(inp["mla_kv_norm_g"]))
    put("psc", _np_layout_vec(inp["pool_scale"]))
    put("convw", _np_layout_vec(inp["conv_w"]))
    put("convb", _np_layout_vec(inp["conv_b"]))
    hp = np.arange(128) // 32
    dec = np.stack([inp["ret_decay_f"][:, hp], inp["ret_decay_b"][:, hp]], 1)
    put("dec", np.moveaxis(dec, -1, 0))
    sel = np.zeros(32, np.float32)
    for r in range(4):
        sel[0 + r] = 2048.0 * (j - 1 - r) if r < j else 0.0
        sel[4 + r] = 1.0 if r < j else 0.0
        sel[8 + r] = 2048.0 * (r - j - 1) if r > j else 0.0
        sel[12 + r] = 1.0 if r > j else 0.0
        sel[20 + r] = 1.0 if r == j - 1 else 0.0
        sel[24 + r] = 1.0 if r == j + 1 else 0.0
    sel[16] = 2048.0 * j
    sel[17] = 2048.0 * (3 - j)
    put("sel", np.tile(sel[None], (128, 1)))
    ws = np.array([[2, 4], [8, 16]], np.float32)
    wp = np.stack([ws[c][(np.arange(128) // 64)] for c in range(2)], -1)
    put("invw", 1.0 / wp)

    def poolcorr(T, t0, n):
        out = np.ones((128, 2, 16), np.float32)
        for c in range(2):
            for p in range(128):
                w = int(wp[p, c])
                for i in range(8):
                    for side, tl in ((0, i), (1, n - 8 + i)):
                        t = t0 + tl
                        lo = min(max(t - w // 2, 0), T)
                        hi = min(max(t + w - w // 2, 0), T)
                        out[p, c, side * 8 + i] = w / float(hi - lo)
        return out

    put("poolc", poolcorr(4 * TL, j * TL, TL))
    put("poolcc", poolcorr(TC, 0, TC))
    pp = np.arange(128, dtype=np.float32)
    put("pcol", np.stack([pp, 127.0 - pp], -1))
    hf = np.arange(128) // 32
    declow = np.stack([inp["ret_decay_f"][:, hf], inp["ret_decay_b"][:, hf]], 1)
    put("declow", np.tile(declow.reshape(1, -1), (128, 1)))
    qq = np.arange(128, dtype=np.float32)[None, :]
    kk = np.arange(128, dtype=np.float32)[:, None]
    put("dpos", np.maximum(qq - kk, 0))
    put("dneg", np.maximum(kk - qq, 0))
    put("mskf", (qq >= kk).astype(np.float32))
    put("mskb", (kk > qq).astype(np.float32))
    put("bmask", ((np.arange(128)[:, None] // 32) == (np.arange(256)[None, :] // 64)).astype(np.float32))
    put("posp1", np.tile(qq + 1.0, (128, 1)))
    put("posm", np.tile(128.0 - qq, (128, 1)))
    cc = np.arange(16, dtype=np.float32)
    put("c128", np.tile(128.0 * cc[None], (128, 1)))
    put("c128r", np.tile(128.0 * (15 - cc)[None], (128, 1)))
    put("c128c", np.tile(np.array([[0.0, 128.0]], np.float32), (128, 1)))
    put("c128cr", np.tile(np.array([[128.0, 0.0]], np.float32), (128, 1)))
    return P


def rope_tables(t0, n):
    pos = np.arange(t0, t0 + n)
    row = (pos // 64).astype(np.float32)
    col = (pos % 64).astype(np.float32)
    inv = (10000.0 ** (-np.arange(8, dtype=np.float32) / 8)).astype(np.float32)
    ang = np.concatenate([row[:, None] * inv, col[:, None] * inv], -1)
    cos = np.cos(ang).astype(np.float32).T
    sin = np.sin(ang).astype(np.float32).T
    cs = np.tile(np.concatenate([cos, cos], 0), (4, 1))
    sn = np.tile(np.concatenate([sin, sin], 0), (4, 1))
    return np.ascontiguousarray(cs), np.ascontiguousarray(sn)


class Arena:
    def __init__(self, lo, hi, rnd=512):
        self.lo, self.hi, self.cur, self.rnd = lo, hi, lo, rnd

    def alloc(self, nbytes):
        nbytes = (nbytes + self.rnd - 1) // self.rnd * self.rnd
        o = self.cur
        assert o + nbytes <= self.hi, ("arena overflow", o, nbytes, self.hi)
        self.cur = o + nbytes
        return o

    def reset(self, to=None):
        self.cur = self.lo if to is None else to


def build_program(dbg=False, stop_after=None, skip_ctx=False, no_coll=False):
    nc = bass.Bass("TRN2", target_bir_lowering=False)
    S = Sched()
    dumps = []

    def din(name, shape, dt=F32):
        return nc.dram_tensor(name, list(shape), dt, kind="ExternalInput").ap()

    xT_d = din("xT", [D, TL])
    ctxT_d = din("ctxT", [D, TC])
    cpk_d = din("cpk", [128, CP_N])
    cs_d = din("cs", [128, TL])
    sn_d = din("sn", [128, TL])
    ident_d = din("ident", [128, 128])
    wmp_d = din("w_mod_p", [3, D, D])
    w_in_d = din("w_in", [DEPTH, D, INW])
    w_uq_d = din("w_uq", [DEPTH, 384, 768])
    w_ukv_d = din("w_ukv", [DEPTH, 256, 1024])
    pool_w_d = din("pool_w", [DEPTH, 4, 64, 64])
    w_out_d = din("w_out", [DEPTH, D, D])
    w_up_d = din("w_up", [DEPTH, D, 2 * DFF])
    w_down_d = din("w_down", [DEPTH, DFF, D])
    outT_d = nc.dram_tensor("outT", [D, TL], F32, kind="ExternalOutput").ap()
    xsp_d = nc.dram_tensor("xspill", [D, TL], F32).ap()
    expM = nc.dram_tensor("expM", [128, 48], F32).ap()
    gM = nc.dram_tensor("gM", [512, 48], F32).ap()
    expA = [nc.dram_tensor("expA%d" % l, [256, TL], BF16).ap() for l in range(DEPTH)]
    gA = [nc.dram_tensor("gA%d" % l, [4 * 256, TL], BF16).ap() for l in range(DEPTH)]
    expKn = [[nc.dram_tensor("expKn%d_%d" % (l, t), [256, TL], BF16).ap() for t in range(2)] for l in range(DEPTH)]
    gKn = [[nc.dram_tensor("gKn%d_%d" % (l, t), [1024, TL], BF16).ap() for t in range(2)] for l in range(DEPTH)]
    expV = [[nc.dram_tensor("expV%d_%d" % (l, t), [1024, 512], BF16).ap() for t in range(2)] for l in range(DEPTH)]
    gV = [[nc.dram_tensor("gV%d_%d" % (l, t), [4096, 512], BF16).ap() for t in range(2)] for l in range(DEPTH)]
    ctxKn = [nc.dram_tensor("ctxKn%d" % l, [512, TC], BF16).ap() for l in range(DEPTH)]
    ctxV = [nc.dram_tensor("ctxV%d" % l, [TC, 512], BF16).ap() for l in range(DEPTH)]
    expK = [nc.dram_tensor("expK%d" % l, [32, TL], BF16).ap() for l in range(DEPTH)]
    gK = [nc.dram_tensor("gK%d" % l, [4 * 32, TL], BF16).ap() for l in range(DEPTH)]
    ctxkv = [nc.dram_tensor("ctxkv%d" % l, [288, TC], BF16).ap() for l in range(DEPTH)]
    expB = [nc.dram_tensor("expB%d" % l, [128, 544], F32).ap() for l in range(DEPTH)]
    gB = [nc.dram_tensor("gB%d" % l, [512, 544], F32).ap() for l in range(DEPTH)]
    expC = [nc.dram_tensor("expC%d" % l, [128, 16], F32).ap() for l in range(DEPTH)]
    gC = [nc.dram_tensor("gC%d" % l, [512, 16], F32).ap() for l in range(DEPTH)]

    TOTAL = 212480
    big = nc.alloc_sbuf_tensor("big", [128, TOTAL], U8)
    ps = nc.alloc_psum_tensor("ps", [128, 7, 512], F32)
    pT = nc.alloc_psum_tensor("pT", [128, 1024], BF16)

    def V(off, shape, dt):
        n = int(np.prod(shape)) * mybir.dt.size(dt)
        assert off % 4 == 0
        ap = big[:, off:off + n].bitcast(dt)
        if len(shape) == 2:
            ap = ap.rearrange("p (a b) -> p a b", b=shape[1])
        elif len(shape) == 3:
            ap = ap.rearrange("p (a b c) -> p a b c", b=shape[1], c=shape[2])
        elif len(shape) == 4:
            ap = ap.rearrange("p (a b c d) -> p a b c d", b=shape[1], c=shape[2], d=shape[3])
        return ap

    PA = Arena(0, 50688, 64)
    XA = Arena(50688, 116224)
    YA = Arena(116224, TOTAL)

    def alloc(arena, shape, dt):
        return V(arena.alloc(int(np.prod(shape)) * mybir.dt.size(dt)), shape, dt)

    def MM(out, lhsT, rhs, start=True, stop=True, tp=None):
        if tp is None:
            S.add("pe", lambda e: e.matmul(out, lhsT, rhs, start=start, stop=stop), reads=[lhsT, rhs], writes=[out])
        else:
            S.add("pe", lambda e: e.matmul(out, lhsT, rhs, start=start, stop=stop, tile_position=tp),
                  reads=[lhsT, rhs], writes=[out])

    def TR(out, in_, ident):
        S.add("pe", lambda e: e.transpose(out, in_, ident), reads=[in_, ident], writes=[out])

    def ACT(out, in_, func, bias=None, scale=None, q="act"):
        kw = {}
        rd = [in_]
        if bias is not None:
            kw["bias"] = bias
            if not isinstance(bias, float):
                rd.append(bias)
        if scale is not None:
            kw["scale"] = scale
            if not isinstance(scale, float):
                rd.append(scale)
        S.add(q, lambda e: e.activation(out=out, in_=in_, func=func, **kw), reads=rd, writes=[out])

    def TT(q, out, a, b, op):
        S.add(q, lambda e: e.tensor_tensor(out, a, b, op), reads=[a, b], writes=[out])

    def TS1(q, out, a, s, op):
        rd = [a] + ([] if isinstance(s, float) else [s])
        S.add(q, lambda e: e.tensor_single_scalar(out, a, s, op), reads=rd, writes=[out])

    def TS2(q, out, a, s1, s2, op0, op1):
        rd = [a] + [x for x in (s1, s2) if not isinstance(x, float)]
        S.add(q, lambda e: e.tensor_scalar(out, a, s1, s2, op0, op1), reads=rd, writes=[out])

    def STT(q, out, a, sc, b, op0, op1):
        rd = [a, b] + ([] if isinstance(sc, float) else [sc])
        S.add(q, lambda e: e.scalar_tensor_tensor(out, a, sc, b, op0, op1), reads=rd, writes=[out])

    def CP(q, out, a):
        if q == "act":
            ACT(out, a, AF.Identity)
        else:
            S.add(q, lambda e: e.tensor_copy(out, a), reads=[a], writes=[out])

    def RCP(out, a):
        S.add("dve", lambda e: e.reciprocal(out, a), reads=[a], writes=[out])

    def MSET(q, ap, val):
        S.add(q, lambda e: e.memset(ap, val), reads=[], writes=[ap])

    def RSUM(out, a):
        S.add("dve", lambda e: e.reduce_sum(out, a, AX.X), reads=[a], writes=[out])

    def DMA(q, out, in_, reads, writes):
        S.add(q, lambda e: e.dma_start(out=out, in_=in_), reads=reads, writes=writes, kind="d")

    def LOAD(q, out, in_, key=None):
        DMA(q, out, in_, [key] if key else [], [out])

    def STORE(q, out, in_, key):
        DMA(q, out, in_, [in_], [key])

    def COLL(ins, outs, kin, kout):
        if no_coll:
            DMA("sp", outs[0:ins.shape[0], :], ins, [kin], [kout])
            return
        S.add("pool", lambda e: e.collective_compute(
            "AllGather", ALU.bypass, replica_groups=[[0, 1, 2, 3], [4, 5, 6, 7]],
            ins=[ins.opt()], outs=[outs.opt()]), reads=[kin], writes=[kout], kind="cc")

    def dump(name, ap, shape, dt=F32):
        if not dbg:
            return
        d = nc.dram_tensor("dbg_" + name, list(ap.shape), dt, kind="ExternalOutput").ap()
        STORE("sp", d, ap, "dbg_" + name)
        dumps.append("dbg_" + name)

    _pb = [0]

    def pbank():
        b = _pb[0]
        _pb[0] = (b + 1) % 7
        return b

    cpk = alloc(PA, [CP_N], F32)

    def C(name, *shape):
        o, n = CP_OFF[name]
        ap = cpk[:, o:o + n]
        if len(shape) == 2:
            ap = ap.rearrange("p (a b) -> p a b", b=shape[1])
        elif len(shape) == 3:
            ap = ap.rearrange("p (a b c) -> p a b c", b=shape[1], c=shape[2])
        elif len(shape) == 4:
            ap = ap.rearrange("p (a b c d) -> p a b c d", b=shape[1], c=shape[2], d=shape[3])
        return ap

    ones_bf = alloc(PA, [128], BF16)
    ident_bf = alloc(PA, [128], BF16)
    ones_f = alloc(PA, [128], F32)
    epsc = alloc(PA, [1], F32)
    zero_f = alloc(PA, [16], F32)
    modv = alloc(PA, [DEPTH, 6, KC, 2], F32)
    Amod = alloc(PA, [DEPTH, 2, KC, 2], F32)
    scv = alloc(PA, [KC, 2], F32)
    lgp = alloc(PA, [2], F32)
    lgrow = alloc(PA, [2, 128], F32)
    etmp = alloc(PA, [2, 128], F32)
    etmp2 = alloc(PA, [2, 128], F32)
    qdec = alloc(PA, [2, 128], F32)
    kdec = alloc(PA, [2, 128], F32)
    cdec = alloc(PA, [2], F32)
    gpow = alloc(PA, [2, 16], F32)
    gpowc = alloc(PA, [2, 2], F32)
    coef = alloc(PA, [2, 4], F32)
    coefc = alloc(PA, [2], F32)
    DTm = alloc(PA, [4, 128], BF16)
    dt1 = alloc(PA, [128], F32)
    dt2 = alloc(PA, [128], F32)
    sctx = alloc(PA, [2, 256], F32)
    Sin = alloc(PA, [2, 256], F32)
    Sst = alloc(PA, [2, 256], F32)
    PW = alloc(PA, [2, 128], BF16)
    cs = alloc(PA, [TL], F32)
    sn = alloc(PA, [TL], F32)
    xcT = alloc(PA, [KC, TC], F32)
    xT = alloc(XA, [KC, TL], F32)

    LOAD("sp", cpk, cpk_d)
    LOAD("sp", cs, cs_d)
    LOAD("sp", sn, sn_d)
    LOAD("pool", ident_bf, ident_d)
    MSET("pool", ones_bf, 1.0)
    MSET("pool", ones_f, 1.0)
    MSET("pool", epsc, EPS)
    MSET("pool", zero_f, 0.0)
    ACT(scv, C("scin", KC, 2), AF.Silu)
    bm = C("bm", DEPTH, 6, KC, 2)
    ymark = YA.cur
    slabs = [alloc(YA, [KC, 1024], F32) for _ in range(2)]
    expMs = alloc(YA, [48], F32)
    for i in range(3):
        slab = slabs[i % 2]
        LOAD("sp", slab, wmp_d[i].rearrange("(k p) n -> p k n", p=128))
        b = pbank()
        for oc in range(KC):
            for kc in range(KC):
                MM(ps[:, b, oc * 2:oc * 2 + 2], slab[:, kc, oc * 128:(oc + 1) * 128], scv[:, kc, :],
                   start=(kc == 0), stop=(kc == KC - 1))
        CP("dve", expMs[:, i * 16:(i + 1) * 16], ps[:, b, 0:16])
        if i == 1:
            LOAD("sp", xcT, ctxT_d.rearrange("(k p) n -> p k n", p=128))
    for tb in range(4):
        LOAD("sp", xT[:, :, tb * 512:(tb + 1) * 512],
             xT_d.rearrange("(k p) n -> p k n", p=128)[:, :, tb * 512:(tb + 1) * 512])
    STORE("sp", expM, expMs, "expM")
    COLL(expM, gM, "expM", "gM")
    mflat = modv.rearrange("p a b c d -> p (a b c d)")
    LOAD("sp", mflat.rearrange("p (r n) -> p r n", n=48), gM.rearrange("(r p) n -> p r n", p=128), "gM")
    TT("dve", mflat, mflat, bm.rearrange("p a b c d -> p (a b c d)"), ALU.add)
    YA.reset(ymark)
    n1g = C("n1g", DEPTH, KC)
    n2g = C("n2g", DEPTH, KC)
    for l in range(DEPTH):
        for wh, (gn, s6) in enumerate(((n1g, 1), (n2g, 4))):
            for m in range(2):
                STT("dve", Amod[:, l, wh, :, m], modv[:, l, s6, :, m], 1.0, gn[:, l, :], ALU.add, ALU.mult)
    dump("modv", modv, [128, DEPTH * 6 * KC * 2])
    if stop_after == "prologue":
        S.add("sp", None, reads=["outT"] + dumps, writes=[])
        S.emit(nc)
        return nc, dumps

    def mA(l, wh, kc, m):
        return Amod[:, l, wh, kc, m:m + 1]

    def mV(l, s6, kc, m):
        return modv[:, l, s6, kc, m:m + 1]

    def norm_mod(xsrc, NT, TB, l, wh, m, hdst, col0, tmpA, order=None, after=None):
        sq = [alloc(tmpA, [TB], BF16) for _ in range(2)]
        rs = [alloc(tmpA, [TB], F32) for _ in range(2)]
        tm = [alloc(tmpA, [TB], F32) for _ in range(2)]
        s_sh = 0 if wh == 0 else 3
        for ti, tb in enumerate(order if order is not None else range(NT // TB)):
            if after and ti in after:
                after[ti]()
            cols = slice(tb * TB, (tb + 1) * TB)
            b = pbank()
            for kc in range(KC):
                s = sq[kc % 2]
                if kc % 2 == 0:
                    TT("pool", s, xsrc[:, kc, cols], xsrc[:, kc, cols], ALU.mult)
                else:
                    ACT(s, xsrc[:, kc, cols], AF.Square)
                MM(ps[:, b, 0:TB], ones_bf, s, start=(kc == 0), stop=(kc == KC - 1))
            r = rs[ti % 2]
            ACT(r, ps[:, b, 0:TB], AF.Sqrt, bias=epsc[:, 0:1], scale=1.0 / D)
            RCP(r, r)
            for kc in range(KC):
                t = tm[kc % 2]
                STT("dve", t, xsrc[:, kc, cols], mA(l, wh, kc, m), r, ALU.mult, ALU.mult)
                ACT(hdst[:, kc, col0 + tb * TB: col0 + (tb + 1) * TB], t, AF.Identity, bias=mV(l, s_sh, kc, m), scale=1.0)

    def neg_log1p_exp_neg(dst, src, n):
        e = etmp[:, 0, 0:n] if n <= 128 else etmp.rearrange("p a b -> p (a b)")[:, 0:n]
        t = etmp2[:, 0, 0:n] if n <= 128 else etmp2.rearrange("p a b -> p (a b)")[:, 0:n]
        ACT(e, src, AF.Exp, scale=-1.0)
        TS2("dve", t, e, -0.2, 0.25, ALU.mult, ALU.add)
        TT("dve", t, t, e, ALU.mult)
        TS2("dve", t, t, -1.0, 1.0 / 3.0, ALU.mult, ALU.add)
        TT("dve", t, t, e, ALU.mult)
        TS2("dve", t, t, -1.0, 0.5, ALU.mult, ALU.add)
        TT("dve", t, t, e, ALU.mult)
        TS2("dve", t, t, -1.0, 1.0, ALU.mult, ALU.add)
        TT("dve", t, t, e, ALU.mult)
        TS1("dve", dst, t, -1.0, ALU.mult)

    def ret_tables(l):
        neg_log1p_exp_neg(lgp, C("dec", DEPTH, 2)[:, l, :], 2)
        neg_log1p_exp_neg(lgrow.rearrange("p a b -> p (a b)"), C("declow", DEPTH, 256)[:, l, :], 256)
        sel = C("sel")
        for d in range(2):
            lg = lgp[:, d:d + 1]
            ACT(qdec[:, d, :], C("posp1") if d == 0 else C("posm"), AF.Exp, scale=lg)
            ACT(kdec[:, d, :], lgrow[:, d, :], AF.Exp, scale=C("pcol")[:, 1:2] if d == 0 else C("pcol")[:, 0:1])
            ACT(gpow[:, d, :], C("c128") if d == 0 else C("c128r"), AF.Exp, scale=lg)
            ACT(gpowc[:, d, :], C("c128c") if d == 0 else C("c128cr"), AF.Exp, scale=lg)
            ACT(coef[:, d, :], sel[:, 8 * d:8 * d + 4], AF.Exp, scale=lg)
            TT("dve", coef[:, d, :], coef[:, d, :], sel[:, 8 * d + 4:8 * d + 8], ALU.mult)
            ACT(coefc[:, d:d + 1], sel[:, 16 + d:17 + d], AF.Exp, scale=lg)
        CP("dve", cdec[:, 0:1], gpow[:, 0, 1:2])
        CP("dve", cdec[:, 1:2], gpow[:, 1, 14:15])
        for h in range(4):
            ACT(dt1, C("dpos"), AF.Exp, scale=lgrow[:, 0, h * 32:h * 32 + 1])
            TT("dve", dt1, dt1, C("mskf"), ALU.mult)
            ACT(dt2, C("dneg"), AF.Exp, scale=lgrow[:, 1, h * 32:h * 32 + 1])
            TT("dve", dt2, dt2, C("mskb"), ALU.mult)
            TT("dve", DTm[:, h, :], dt1, dt2, ALU.add)

    def build_rot(dst, src, nb):
        d4 = dst.rearrange("p (n t s) -> p n t s", t=2, s=16)
        s4 = src.rearrange("p (n t s) -> p n t s", t=2, s=16)
        TS1("pool", d4[:, :, 0, :], s4[:, :, 1, :], -1.0, ALU.mult)
        CP("pool", d4[:, :, 1, :], s4[:, :, 0, :])

    def inproj(l, hT, col0, NT, TB, rope, o, WA, full=True):
        wst = [alloc(WA, [KC, 512], BF16) for _ in range(2)]
        wrot = alloc(WA, [KC, 256], BF16)
        sq = [alloc(WA, [TB], BF16) for _ in range(2)]
        rs = [alloc(WA, [TB], F32) for _ in range(2)]
        t1 = [alloc(WA, [TB], F32) for _ in range(2)]
        t2 = [alloc(WA, [TB], F32) for _ in range(2)]
        ntb = NT // TB
        wi = [0]

        groups = [(1152, 288), (0, 256), (256, 512)] + ([(768, 384), (1440, 256)] if full else [])
        pend = {}

        def issue(gi_):
            if gi_ < len(groups):
                c0_, n_ = groups[gi_]
                w_ = wst[gi_ % 2]
                LOAD("pool", w_[:, :, 0:n_], w_in_d[l, :, c0_:c0_ + n_].rearrange("(k p) n -> p k n", p=128))
                pend[(c0_, n_)] = w_

        issue(0)

        def loadw(c0, n):
            gi_ = groups.index((c0, n))
            w = pend[(c0, n)]
            issue(gi_ + 1)
            return w

        def chain(bank, pr, lhs_fn, rhs_fn, n):
            for kc in range(KC):
                MM(ps[pr, bank, 0:n], lhs_fn(kc), rhs_fn(kc), start=(kc == 0), stop=(kc == KC - 1))

        def hcols(tb):
            return slice(col0 + tb * TB, col0 + (tb + 1) * TB)

        rr = [0]

        def rope_evac(dst, pa, pb_, cols, prt, scale):
            i = rr[0] % 2
            rr[0] += 1
            STT("dve", t1[i][prt, :], pa, scale, cs[prt, cols], ALU.mult, ALU.mult)
            STT("dve", t2[i][prt, :], pb_, scale, sn[prt, cols], ALU.mult, ALU.mult)
            TT("pool", dst, t1[i][prt, :], t2[i][prt, :], ALU.add)

        w = loadw(1152, 288)
        if rope:
            for kc in range(KC):
                build_rot(wrot[:, kc, 0:32], w[:, kc, 256:288], 1)
        for tb in range(ntb):
            cols = slice(tb * TB, (tb + 1) * TB)
            bk = [pbank(), pbank()]
            for c in range(2):
                chain(bk[c], slice(0, 128), lambda kc: w[:, kc, c * 128:(c + 1) * 128], lambda kc: hT[:, kc, hcols(tb)], TB)
            bs = pbank()
            for c in range(2):
                ACT(sq[c], ps[:, bk[c], 0:TB], AF.Square)
                MM(ps[:, bs, 0:TB], ones_bf, sq[c], start=(c == 0), stop=(c == 1))
            r = rs[tb % 2]
            ACT(r, ps[:, bs, 0:TB], AF.Sqrt, bias=epsc[:, 0:1], scale=1.0 / 256)
            RCP(r, r)
            for c in range(2):
                STT("dve", o["ckv"][:, c, cols], ps[:, bk[c], 0:TB], C("kvng", DEPTH, 2)[:, l, c:c + 1], r, ALU.mult, ALU.mult)
            ba = pbank()
            p32 = slice(0, 32)
            chain(ba, p32, lambda kc: w[:, kc, 256:288], lambda kc: hT[:, kc, hcols(tb)], TB)
            if rope:
                bb = pbank()
                chain(bb, p32, lambda kc: wrot[:, kc, 0:32], lambda kc: hT[:, kc, hcols(tb)], TB)
                rope_evac(o["kr"][p32, cols], ps[p32, ba, 0:TB], ps[p32, bb, 0:TB], cols, p32, 1.0)
            else:
                ACT(o["kr"][p32, cols], ps[p32, ba, 0:TB], AF.Identity)
        if stop_after == "ip1":
            return
        w = loadw(0, 256)
        if rope:
            for kc in range(KC):
                build_rot(wrot[:, kc, :], w[:, kc, 0:256], 8)
        for tb in range(ntb):
            cols = slice(tb * TB, (tb + 1) * TB)
            for wh, dst, scl in ((0, o["q"], 1.0), (1, o["k"], RET_KSCALE)):
                if wh == 0 and not full:
                    continue
                ba = pbank()
                chain(ba, slice(0, 128), lambda kc: w[:, kc, wh * 128:(wh + 1) * 128], lambda kc: hT[:, kc, hcols(tb)], TB)
                if rope:
                    bb = pbank()
                    chain(bb, slice(0, 128), lambda kc: wrot[:, kc, wh * 128:(wh + 1) * 128], lambda kc: hT[:, kc, hcols(tb)], TB)
                    rope_evac(dst[:, cols], ps[:, ba, 0:TB], ps[:, bb, 0:TB], cols, slice(0, 128), scl)
                else:
                    ACT(dst[:, cols], ps[:, ba, 0:TB], AF.Identity, scale=scl)
        if stop_after == "ip2":
            return
        w = loadw(256, 512)
        for t in range(NT // 128):
            b = pbank()
            chain(b, slice(0, 128), lambda kc: hT[:, kc, col0 + t * 128: col0 + (t + 1) * 128], lambda kc: w[:, kc, 0:512], 512)
            CP("dve", o["vg"][:, t, 0:256], ps[:, b, 0:256])
            if full:
                ACT(o["vg"][:, t, 256:512], ps[:, b, 256:512], AF.Silu)
        if not full:
            return
        if stop_after == "ip3":
            return
        w = loadw(768, 384)
        for tb in range(ntb):
            cols = slice(tb * TB, (tb + 1) * TB)
            bk = [pbank(), pbank(), pbank()]
            for c in range(3):
                chain(bk[c], slice(0, 128), lambda kc: w[:, kc, c * 128:(c + 1) * 128], lambda kc: hT[:, kc, hcols(tb)], TB)
            bs = pbank()
            for c in range(3):
                ACT(sq[c % 2], ps[:, bk[c], 0:TB], AF.Square)
                MM(ps[:, bs, 0:TB], ones_bf, sq[c % 2], start=(c == 0), stop=(c == 2))
            r = rs[tb % 2]
            ACT(r, ps[:, bs, 0:TB], AF.Sqrt, bias=epsc[:, 0:1], scale=1.0 / 384)
            RCP(r, r)
            for c in range(3):
                STT("dve", o["cq"][:, c, cols], ps[:, bk[c], 0:TB], C("qng", DEPTH, 3)[:, l, c:c + 1], r, ALU.mult, ALU.mult)
        if stop_after == "ip4":
            return
        w = loadw(1440, 256)
        for tb in range(ntb):
            cols = slice(8 + tb * TB, 8 + (tb + 1) * TB)
            for c in range(2):
                b = pbank()
                chain(b, slice(0, 128), lambda kc: w[:, kc, c * 128:(c + 1) * 128], lambda kc: hT[:, kc, hcols(tb)], TB)
                ACT(o["xp"][:, c, cols], ps[:, b, 0:TB], AF.Identity)

    def ret_ktok(kT, ktok, NT):
        nch = NT // 128
        for c0 in range(0, nch, 8):
            n = min(8, nch - c0)
            for i in range(n):
                TR(pT[:, i * 128:(i + 1) * 128], kT[:, (c0 + i) * 128:(c0 + i + 1) * 128], ident_bf)
            CP("dve", ktok[:, c0:c0 + n, :], pT[:, 0:n * 128].rearrange("p (a b) -> p a b", b=128))

    class RetTmp:
        def __init__(self, RA, nch):
            self.ks = [alloc(RA, [128], BF16) for _ in range(2)]
            self.ut = [alloc(RA, [256], F32) for _ in range(2)]
            self.PTm = [alloc(RA, [4, 128], BF16) for _ in range(2)]
            self.qs = [alloc(RA, [2, 128], BF16) for _ in range(2)]
            self.osq = [alloc(RA, [256], F32) for _ in range(2)]
            self.ss = [alloc(RA, [4], F32) for _ in range(2)]
            self.ytok = [alloc(RA, [256], BF16) for _ in range(2)]
            self.Sf = [alloc(RA, [256], BF16) for _ in range(2)]
            self.Sb = alloc(RA, [nch, 256], BF16)
            self.SfA = alloc(RA, [nch, 256], BF16)
            self.kpad = [alloc(RA, [4, 128], BF16) for _ in range(2)]
            for kp in self.kpad:
                MSET("pool", kp.rearrange("p a b -> p (a b)"), 0.0)
            self.n = 0

    def ret_update(T, d, c, ktok, vg):
        i = T.n % 2
        T.n += 1
        TT("dve", T.ks[i], ktok[:, c, :], kdec[:, d, :], ALU.mult)
        b = pbank()
        MM(ps[:, b, 0:256], T.ks[i], vg[:, c, 0:256])
        TT("dve", T.ut[i], ps[:, b, 0:256], C("bmask"), ALU.mult)
        STT("dve", Sst[:, d, :], Sst[:, d, :], cdec[:, d:d + 1], T.ut[i], ALU.mult, ALU.add)

    def ret_pass1(T, ktok, vg, NT, dstF, dstB):
        nch = NT // 128
        MSET("dve", Sst.rearrange("p a b -> p (a b)"), 0.0)
        for i in range(nch):
            cf, cbk = i, nch - 1 - i
            CP("dve", T.SfA[:, cf, :], Sst[:, 0, :])
            ret_update(T, 0, cf, ktok, vg)
            CP("dve", T.Sb[:, cbk, :], Sst[:, 1, :])
            ret_update(T, 1, cbk, ktok, vg)
        CP("dve", dstF, Sst[:, 0, :])
        CP("dve", dstB, Sst[:, 1, :])

    def ret_pass2(T, qT, kT, ktok, vg, NT, S0, mix, col0, hooks=None):
        nch = NT // 128
        if S0 is not None:
            for c in range(nch):
                STT("dve", T.SfA[:, c, :], S0[:, 0, :], gpow[:, 0, c:c + 1], T.SfA[:, c, :], ALU.mult, ALU.add)
                STT("dve", T.Sb[:, c, :], S0[:, 1, :], gpow[:, 1, c:c + 1], T.Sb[:, c, :], ALU.mult, ALU.add)
        bos = {}

        def H1(c):
            i = c % 2
            ch = slice(c * 128, (c + 1) * 128)
            if hooks and c in hooks:
                hooks[c]()
            ba = pbank()
            for h in range(4):
                hp = slice(32 * h, 32 * h + 32)
                CP("act", T.kpad[i][hp, h, :], kT[hp, ch])
            for h in range(4):
                MM(ps[:, ba, h * 128:(h + 1) * 128], T.kpad[i][:, h, :], qT[:, ch])
            TT("dve", T.PTm[i], ps[:, ba, 0:512].rearrange("p (a b) -> p a b", b=128), DTm, ALU.mult)
            TT("dve", T.qs[i][:, 0, :], qT[:, ch], qdec[:, 0, :], ALU.mult)
            TT("dve", T.qs[i][:, 1, :], qT[:, ch], qdec[:, 1, :], ALU.mult)
            bo = pbank()
            bos[c] = bo
            for h in range(4):
                hv = slice(h * 64, (h + 1) * 64)
                MM(ps[:, bo, hv], T.PTm[i][:, h, :], vg[:, c, hv], start=True, stop=False)
                MM(ps[:, bo, hv], T.qs[i][:, 0, :], T.SfA[:, c, hv], start=False, stop=False)
                MM(ps[:, bo, hv], T.qs[i][:, 1, :], T.Sb[:, c, hv], start=False, stop=True)

        def H2(c):
            i = c % 2
            bo = bos[c]
            ACT(T.osq[i], ps[:, bo, 0:256], AF.Square)
            RSUM(T.ss[i], T.osq[i].rearrange("p (a b) -> p a b", b=64))
            ACT(T.ss[i], T.ss[i], AF.Sqrt, bias=epsc[:, 0:1], scale=1.0 / 64)
            RCP(T.ss[i], T.ss[i])
            for h in range(4):
                hv = slice(h * 64, (h + 1) * 64)
                STT("dve", T.ytok[i][:, hv], ps[:, bo, hv], T.ss[i][:, h:h + 1], vg[:, c, 256 + h * 64:256 + (h + 1) * 64],
                    ALU.mult, ALU.mult)
            TR(pT[:, 0:128], T.ytok[i][:, 0:128], ident_bf)
            TR(pT[:, 128:256], T.ytok[i][:, 128:256], ident_bf)
            CP("act", mix[:, 0:2, col0 + c * 128: col0 + (c + 1) * 128], pT[:, 0:256].rearrange("p (a b) -> p a b", b=128))

        H1(0)
        for c in range(nch):
            if c + 1 < nch:
                H1(c + 1)
            H2(c)

    def pool_mix(l, xp, NT, TB, mix, col0, corr, RA, RA2=None):
        PADW = NT + 16
        A = alloc(RA, [2, PADW], F32)
        B = alloc(RA, [2, PADW], F32)
        dT = alloc(RA2 if RA2 is not None else RA, [2, NT], BF16)
        TT("dve", A[:, :, 0:PADW - 1], xp[:, :, 0:PADW - 1], xp[:, :, 1:PADW], ALU.add)
        TT("dve", B[:, :, 0:PADW - 3], A[:, :, 0:PADW - 3], A[:, :, 2:PADW - 1], ALU.add)
        TT("dve", A[:, 1, 0:PADW - 7], B[:, 1, 0:PADW - 7], B[:, 1, 4:PADW - 3], ALU.add)
        TT("dve", B[:, 1, 0:PADW - 15], A[:, 1, 0:PADW - 15], A[:, 1, 8:PADW - 7], ALU.add)
        lo, hi = slice(0, 64), slice(64, 128)
        W = [(lo, 0, A[lo, 0, 7:7 + NT]), (hi, 0, B[hi, 0, 6:6 + NT]), (lo, 1, A[lo, 1, 4:4 + NT]), (hi, 1, B[hi, 1, 0:NT])]
        invw = C("invw")
        for gi, (rows, c, Wg) in enumerate(W):
            TT("dve", Wg[:, 0:8], Wg[:, 0:8], corr[rows, c, 0:8], ALU.mult)
            TT("dve", Wg[:, NT - 8:NT], Wg[:, NT - 8:NT], corr[rows, c, 8:16], ALU.mult)
            STT("dve", dT[rows, c, :], Wg, invw[rows, c:c + 1], xp[rows, c, 8:8 + NT],
                ALU.mult, ALU.subtract)
        psc = C("psc", DEPTH, 2)
        for tb in range(NT // TB):
            cols = slice(tb * TB, (tb + 1) * TB)
            for c in range(2):
                b = pbank()
                MM(ps[:, b, 0:TB], PW[:, c, :], dT[:, c, cols])
                ACT(mix[:, 6 + c, col0 + tb * TB: col0 + (tb + 1) * TB], ps[:, b, 0:TB], AF.Identity, scale=psc[:, l, c:c + 1])

    pT32 = pT[:, :].bitcast(F32)
    misc = [ps[:, 6, :], pT32]
    _mi = [0]

    def mbank():
        b = misc[_mi[0] % 2]
        _mi[0] += 1
        return b

    def kv_produce(l, ckv, NT, TB, is_ctx, WA):
        wukv = alloc(WA, [2, 1024], BF16)
        wkn = alloc(WA, [2, 512], BF16)
        wv = alloc(WA, [2, 512], BF16)
        stg = [alloc(WA, [512], BF16) for _ in range(4)]
        LOAD("pool", wukv, w_ukv_d[l].rearrange("(k p) n -> p k n", p=128))
        for c in range(2):
            src = wukv[:, c, :].rearrange("p (h f) -> p h f", f=128)
            CP("pool", wkn[:, c, :].rearrange("p (h f) -> p h f", f=64), src[:, :, 0:64])
            CP("pool", wv[:, c, :].rearrange("p (h f) -> p h f", f=64), src[:, :, 64:128])
        n = 0
        for tb in range(NT // TB):
            cols = slice(tb * TB, (tb + 1) * TB)
            for pair in range(4):
                b = pbank()
                for c in range(2):
                    MM(ps[:, b, 0:TB], wkn[:, c, pair * 128:(pair + 1) * 128], ckv[:, c, cols], start=(c == 0), stop=(c == 1))
                sg = stg[n % 4]
                CP("dve" if n % 2 == 0 else "act", sg[:, 0:TB], ps[:, b, 0:TB])
                n += 1
                if is_ctx:
                    STORE("sp", ctxKn[l][pair * 128:(pair + 1) * 128, cols], sg[:, 0:TB], "ctxKn%d" % l)
                else:
                    STORE("sp", expKn[l][pair // 2][(pair % 2) * 128:(pair % 2) * 128 + 128, cols], sg[:, 0:TB], "expKn%d_%d" % (l, pair // 2))
        for t in range(NT // 128):
            b = pbank()
            for c in range(2):
                MM(ps[:, b, 0:512], ckv[:, c, t * 128:(t + 1) * 128], wv[:, c, :], start=(c == 0), stop=(c == 1))
            sg = stg[n % 4]
            CP("dve" if n % 2 == 0 else "act", sg, ps[:, b, 0:512])
            n += 1
            if is_ctx:
                STORE("sp", ctxV[l][t * 128:(t + 1) * 128, :], sg, "ctxV%d" % l)
            else:
                STORE("sp", expV[l][t // 8][(t % 8) * 128:(t % 8) * 128 + 128, :], sg, "expV%d_%d" % (l, t // 8))
        pend = []
        if not is_ctx:
            for t in range(2):
                pend.append(lambda t=t: COLL(expKn[l][t], gKn[l][t], "expKn%d_%d" % (l, t), "gKn%d_%d" % (l, t)))
            for t in range(2):
                pend.append(lambda t=t: COLL(expV[l][t], gV[l][t], "expV%d_%d" % (l, t), "gV%d_%d" % (l, t)))
        return pend

    def load_head_kv(l, h, KTb, Vb, with_latent):
        pair = h // 2
        t = pair // 2
        rowbase = (pair % 2) * 128 + (h % 2) * 64
        LOAD("sp", KTb[0:64, 0:TC], ctxKn[l][h * 64:(h + 1) * 64, :], "ctxKn%d" % l)
        LOAD("sp", Vb[:, 0:2, 0:64], ctxV[l][:, h * 64:(h + 1) * 64].rearrange("(t p) d -> p t d", p=128), "ctxV%d" % l)
        if with_latent:
            for r in range(4):
                LOAD("sp", KTb[0:64, TC + r * TL:TC + (r + 1) * TL], gKn[l][t][r * 256 + rowbase:r * 256 + rowbase + 64, :],
                     "gKn%d_%d" % (l, t))
                for hf in range(2):
                    LOAD("sp", Vb[:, 2 + r * 16 + hf * 8:2 + r * 16 + hf * 8 + 8, 0:64],
                         gV[l][hf][r * 1024:(r + 1) * 1024, h * 64:(h + 1) * 64].rearrange("(t p) d -> p t d", p=128),
                         "gV%d_%d" % (l, hf))

    def mla_weights(l, WA):
        wuq = alloc(WA, [3, 768], BF16)
        wuqr = alloc(WA, [3, 8, 96], BF16)
        wukv = None
        LOAD("pool", wuq, w_uq_d[l].rearrange("(k p) n -> p k n", p=128))
        MSET("pool", wuqr.rearrange("p a b c -> p (a b c)"), 0.0)
        for c in range(3):
            src = wuq[:, c, :].rearrange("p (h f) -> p h f", f=96)[:, :, 64:96].rearrange("p h (t s) -> p h t s", s=16)
            dst = wuqr[:, c, :, 64:96].rearrange("p h (t s) -> p h t s", s=16)
            TS1("pool", dst[:, :, 0, :], src[:, :, 1, :], -1.0, ALU.mult)
            CP("pool", dst[:, :, 1, :], src[:, :, 0, :])
        return wuq, wuqr, wukv

    def kv_blocks(l, with_latent):
        blks = [(ctxkv[l][0:256, :], "ctxkv%d" % l, TC, 0)]
        if with_latent:
            for r in range(4):
                for jj in range(4):
                    blks.append((gA[l][r * 256:r * 256 + 256, jj * 512:(jj + 1) * 512], "gA%d" % l, 512, TC + r * TL + jj * 512))
        return blks

    def kv_head_steps(l, h, KTb, Vb, wukv, blks, stg):
        for bi, (src, key, n, k0) in enumerate(blks):
            s = stg[bi % 2]
            LOAD("sp", s[:, :, 0:n], src.rearrange("(c p) n -> p c n", p=128), key)
            mb = mbank()
            for c in range(2):
                MM(mb[0:64, 0:n], wukv[:, c, h * 128:h * 128 + 64], s[:, c, 0:n], start=(c == 0), stop=(c == 1))
            CP("dve", KTb[0:64, k0:k0 + n], mb[0:64, 0:n])
            mb = mbank()
            nt = n // 128
            for t in range(nt):
                for c in range(2):
                    MM(mb[:, t * 64:(t + 1) * 64], s[:, c, t * 128:(t + 1) * 128], wukv[:, c, h * 128 + 64:h * 128 + 128],
                       start=(c == 0), stop=(c == 1))
            CP("dve", Vb[:, k0 // 128:k0 // 128 + nt, 0:64], mb[:, 0:nt * 64].rearrange("p (a b) -> p a b", b=64))
            yield

    def kr_rows(l, KTb, with_latent):
        LOAD("sp", KTb[64:96, 0:TC], ctxkv[l][256:288, :], "ctxkv%d" % l)
        if with_latent:
            for r in range(4):
                LOAD("sp", KTb[64:96, TC + r * TL:TC + (r + 1) * TL], gK[l][r * 32:r * 32 + 32, :], "gK%d" % l)

    def q_head_steps(l, h, QTb, cq, wuq, wuqr, NT, TB, rope, tq):
        for tb in range(NT // TB):
            cols = slice(tb * TB, (tb + 1) * TB)
            ma = mbank()
            for c in range(3):
                MM(ma[0:96, 0:TB], wuq[:, c, h * 96:(h + 1) * 96], cq[:, c, cols], start=(c == 0), stop=(c == 2))
            CP("dve", QTb[0:64, cols], ma[0:64, 0:TB])
            pr = slice(64, 96)
            if rope:
                mb = mbank()
                for c in range(3):
                    MM(mb[0:96, 0:TB], wuqr[:, c, h, :], cq[:, c, cols], start=(c == 0), stop=(c == 2))
                TT("dve", tq[0][pr, 0:TB], ma[pr, 0:TB], cs[pr, cols], ALU.mult)
                TT("dve", tq[1][pr, 0:TB], mb[pr, 0:TB], sn[pr, cols], ALU.mult)
                TT("pool", QTb[pr, cols], tq[0][pr, 0:TB], tq[1][pr, 0:TB], ALU.add)
            else:
                CP("dve", QTb[pr, cols], ma[pr, 0:TB])
            yield

    def attend(h, QTb, KTb, Vb, nkb, NT, mix, col0, PTs, OTs, dn, bg, st):
        QCW = min(1024, NT)
        Wd = min(512, QCW)
        nh = QCW // Wd
        po = (h % 2) * 64
        pr = slice(po, po + 64)

        def qk(qc, kb):
            sb = kb % 2
            for hf in range(nh):
                q0 = qc * QCW + hf * Wd
                MM(ps[:, 2 * sb + hf, 0:Wd], KTb[0:96, kb * 128:(kb + 1) * 128], QTb[0:96, q0:q0 + Wd])

        for qc in range(NT // QCW):
            for i in range(nkb + 2):
                if i == 3 and st.get("pend") is not None:
                    st["pend"]()
                    st["pend"] = None
                if i < nkb:
                    qk(qc, i)
                    sb = i % 2
                    P = PTs[i % 3]
                    ACT(P[:, 0:QCW].rearrange("p (a b) -> p a b", b=Wd), ps[:, 2 * sb:2 * sb + nh, 0:Wd], AF.Exp, scale=MLA_SCALE)
                if i >= 2:
                    kb = i - 2
                    P = PTs[kb % 3]
                    for hf in range(nh):
                        MM(ps[0:65, 4 + hf, 0:Wd], Vb[:, kb, 0:65], P[:, hf * Wd:(hf + 1) * Wd], start=(kb == 0), stop=(kb == nkb - 1))
                    if bg is not None and kb % 5 == 4:
                        next(bg, None)
            if st.get("pend") is not None:
                st["pend"]()
                st["pend"] = None
            CP("dve", OTs[pr, 0:QCW].rearrange("p (a b) -> p a b", b=Wd), ps[0:64, 4:4 + nh, 0:Wd])
            CP("dve", dn[0:1, 0:QCW].rearrange("p (a b) -> p a b", b=Wd), ps[64:65, 4:4 + nh, 0:Wd])
            RCP(dn[0:1, 0:QCW], dn[0:1, 0:QCW])

            def finish(qc=qc):
                for hf in range(nh):
                    mb = mbank()
                    MM(mb[:, 0:Wd], ones_f[0:1, 0:128], dn[0:1, hf * Wd:(hf + 1) * Wd])
                    q0 = col0 + qc * QCW + hf * Wd
                    TT("dve", mix[pr, 2 + h // 2, q0:q0 + Wd], OTs[pr, hf * Wd:(hf + 1) * Wd], mb[pr, 0:Wd], ALU.mult)

            st["pend"] = finish

    def mla(l, cq, NT, TB, rope, with_latent, mix, col0, WA, KA):
        wuq, wuqr, wukv = mla_weights(l, WA)
        nkeys = NKEY if with_latent else TC
        nkb = nkeys // 128
        QT = [alloc(WA, [NT], BF16) for _ in range(2)]
        tq = [alloc(WA, [TB], F32) for _ in range(2)]
        OTs = alloc(WA, [min(1024, NT)], F32)
        dn = alloc(WA, [min(1024, NT)], F32)
        KT = [alloc(KA, [nkeys], BF16) for _ in range(2)]
        Vb = [alloc(KA, [nkb, 80], BF16) for _ in range(2)]
        PTs = [alloc(KA, [min(1024, NT)], BF16) for _ in range(3)]
        for i in range(2):
            kr_rows(l, KT[i], with_latent)
            MSET("pool", Vb[i][:, :, 64:65], 1.0)

        def prod(h):
            load_head_kv(l, h, KT[h % 2], Vb[h % 2], with_latent)
            for _ in q_head_steps(l, h, QT[h % 2], cq, wuq, wuqr, NT, TB, rope, tq):
                yield

        for _ in prod(0):
            pass
        st = {"pend": None}
        for h in range(8):
            bg = prod(h + 1) if h + 1 < 8 else None
            attend(h, QT[h % 2], KT[h % 2], Vb[h % 2], nkb, NT, mix, col0, PTs, OTs, dn, bg, st)
            if bg is not None:
                for _ in bg:
                    pass
        if st["pend"] is not None:
            st["pend"]()

    def out_proj(wout, mix, col0, NT, TB, xdst, l, m, order=None):
        for tb in (order if order is not None else range(NT // TB)):
            for oc in range(KC):
                b = pbank()
                for kc in range(KC):
                    MM(ps[:, b, 0:TB], wout[:, kc, oc * 128:(oc + 1) * 128], mix[:, kc, col0 + tb * TB: col0 + (tb + 1) * TB],
                       start=(kc == 0), stop=(kc == KC - 1))
                xs = xdst[:, oc, tb * TB:(tb + 1) * TB]
                STT("dve", xs, ps[:, b, 0:TB], mV(l, 2, oc, m), xs, ALU.mult, ALU.add)

    def ffn(l, h2T, NT, PASS, xdst, m, WA, depth=2):
        SBW = min(512, PASS)
        nsb = PASS // SBW
        GH = NG // 2
        actT = alloc(WA, [GH, PASS], BF16)
        wab = [alloc(WA, [2, KC, 128], BF16) for _ in range(depth)]
        wd = [alloc(WA, [GH, 128], BF16) for _ in range(2)]
        acc = [alloc(WA, [2, SBW], F32) for _ in range(2)]
        sact = [alloc(WA, [SBW], F32) for _ in range(2)]
        cw = C("convw", DEPTH, 3, 44)
        cb = C("convb", DEPTH, 44)
        n = [0, 0, 0, 0]
        wab.append(alloc(WA, [2, KC, 128], BF16))
        items = [(p, half, gi) for p in range(NT // PASS) for half in range(2) for gi in range(GH)]

        def load_w(i):
            p, half, gi = items[i]
            g = half * GH + gi
            w = wab[i % (depth + 1)]
            LOAD("pool", w[:, 0, :, :], w_up_d[l, :, g * 128:(g + 1) * 128].rearrange("(k p) n -> p k n", p=128))
            LOAD("pool", w[:, 1, :, :], w_up_d[l, :, DFF + g * 128:DFF + (g + 1) * 128].rearrange("(k p) n -> p k n", p=128))

        def load_wd(half, oc):
            wdd = wd[oc % 2]
            LOAD("pool", wdd, w_down_d[l, half * GH * 128:(half + 1) * GH * 128, oc * 128:(oc + 1) * 128]
                 .rearrange("(g p) n -> p g n", p=128))

        tail = [None]
        for j0 in range(min(depth, len(items))):
            load_w(j0)
        for i, (p, half, gi) in enumerate(items):
            g = half * GH + gi
            w = wab[i % (depth + 1)]
            if i + depth < len(items):
                load_w(i + depth)
            if gi == GH - 1:
                load_wd(half, 0)
            for sb in range(nsb):
                t0 = p * PASS + sb * SBW
                a = acc[n[1] % 2]
                n[1] += 1
                for ab in range(2):
                    ch = g + ab * NG
                    bmn = n[2] % 4
                    n[2] += 1
                    hc = (n[3] % 64) * 4
                    hbk = ps[:, 4, :] if n[3] % 2 == 0 else pT32
                    n[3] += 1
                    for kc in range(KC):
                        MM(ps[:, bmn, 0:SBW], w[:, ab, kc, :], h2T[:, kc, 1 + t0:1 + t0 + SBW],
                           start=(kc == 0), stop=(kc == KC - 1))
                    for kc in range(KC):
                        MM(hbk[:, hc:hc + 2], w[:, ab, kc, :], h2T[:, kc, t0:t0 + SBW + 2:SBW + 1],
                           start=(kc == 0), stop=(kc == KC - 1))
                    main = ps[:, bmn, 0:SBW]
                    aa = a[:, ab, :]
                    ACT(aa, main, AF.Identity, bias=cb[:, l, ch:ch + 1], scale=cw[:, l, 1, ch:ch + 1])
                    STT("dve", aa[:, 1:SBW], main[:, 0:SBW - 1], cw[:, l, 0, ch:ch + 1], aa[:, 1:SBW], ALU.mult, ALU.add)
                    STT("dve", aa[:, 0:1], hbk[:, hc:hc + 1], cw[:, l, 0, ch:ch + 1], aa[:, 0:1], ALU.mult, ALU.add)
                    STT("dve", aa[:, 0:SBW - 1], main[:, 1:SBW], cw[:, l, 2, ch:ch + 1], aa[:, 0:SBW - 1], ALU.mult, ALU.add)
                    STT("dve", aa[:, SBW - 1:SBW], hbk[:, hc + 1:hc + 2], cw[:, l, 2, ch:ch + 1], aa[:, SBW - 1:SBW],
                        ALU.mult, ALU.add)
                if tail[0] is not None:
                    tail[0]()

                def mk_tail(a=a, sa=sact[n[1] % 2], gi=gi, sb=sb):
                    ACT(sa, a[:, 0, :], AF.Silu)
                    TT("pool", actT[:, gi, sb * SBW:(sb + 1) * SBW], sa, a[:, 1, :], ALU.mult)

                tail[0] = mk_tail
            if gi == GH - 1:
                if tail[0] is not None:
                    tail[0]()
                    tail[0] = None
                for oc in range(KC):
                    wdd = wd[oc % 2]
                    if oc + 1 < KC:
                        load_wd(half, oc + 1)
                    for sb in range(nsb):
                        t0 = p * PASS + sb * SBW
                        b = 5 + (oc * nsb + sb) % 2
                        for gj in range(GH):
                            MM(ps[:, b, 0:SBW], wdd[:, gj, :], actT[:, gj, sb * SBW:(sb + 1) * SBW],
                               start=(gj == 0), stop=(gj == GH - 1))
                        xs = xdst[:, oc, t0:t0 + SBW]
                        STT("dve", xs, ps[:, b, 0:SBW], mV(l, 5, oc, m), xs, ALU.mult, ALU.add)

    def load_layer_consts(l):
        ret_tables(l)
        MSET("pool", PW.rearrange("p a b -> p (a b)"), 0.0)
        for c in range(2):
            LOAD("pool", PW[0:64, c, 0:64], pool_w_d[l, 2 * c])
            LOAD("pool", PW[64:128, c, 64:128], pool_w_d[l, 2 * c + 1])

    def load_wout(l, WA):
        wout = alloc(WA, [KC, D], BF16)
        LOAD("pool", wout, w_out_d[l].rearrange("(k p) n -> p k n", p=128))
        return wout

    def ctx_layer(l):
        full = (l < DEPTH - 1) and stop_after not in ("retpool", "mla", "outproj")
        YA.reset()
        NT, TB = TC, TC
        hT = alloc(YA, [KC, NT + 32], BF16)
        o = {"cq": alloc(YA, [3, NT], BF16), "ckv": alloc(YA, [2, NT], BF16), "kr": alloc(YA, [NT], BF16),
             "q": alloc(YA, [NT], BF16), "k": alloc(YA, [NT], BF16), "vg": alloc(YA, [NT // 128, 512], BF16),
             "xp": alloc(YA, [2, NT + 16], F32)}
        ktok = alloc(YA, [NT // 128, 128], BF16)
        ym = YA.cur
        norm_mod(xcT, NT, TB, l, 0, 1, hT, 16, YA)
        YA.reset(ym)
        inproj(l, hT, 16, NT, TB, False, o, YA, full=full)
        YA.reset(ym)
        STORE("sp", ctxkv[l][256:288, :], o["kr"][0:32, :], "ctxkv%d" % l)
        kv_produce(l, o["ckv"], NT, TB, True, YA)
        YA.reset(ym)
        ret_ktok(o["k"], ktok, NT)
        T = RetTmp(YA, NT // 128)
        ret_pass1(T, ktok, o["vg"], NT, sctx[:, 0, :], sctx[:, 1, :])
        if not full:
            return
        ret_pass2(T, o["q"], o["k"], ktok, o["vg"], NT, None, hT, 16)
        for c in range(2):
            MSET("pool", o["xp"][:, c, 0:8], 0.0)
            MSET("pool", o["xp"][:, c, 8 + NT:16 + NT], 0.0)
        pool_mix(l, o["xp"], NT, TB, hT, 16, C("poolcc", 2, 16), YA)
        wout = load_wout(l, YA)
        mla(l, o["cq"], NT, TB, False, False, hT, 16, YA, YA)
        out_proj(wout, hT, 16, NT, TB, xcT, l, 1)
        dump("xc_mid%d" % l, xcT, [128, KC * TC])
        YA.reset()
        h2 = alloc(YA, [KC, NT + 2], BF16)
        MSET("pool", h2[:, :, 0:1], 0.0)
        MSET("pool", h2[:, :, NT + 1:NT + 2], 0.0)
        norm_mod(xcT, NT, TB, l, 1, 1, h2, 1, YA)
        ffn(l, h2, NT, NT, xcT, 1, YA, depth=5)
        dump("xc_out%d" % l, xcT, [128, KC * TC])

    def latent_layer(l):
        NT, TB = TL, 512
        YA.reset()
        B32 = alloc(YA, [KC, NT + 32], BF16)
        cq = alloc(YA, [3, NT], BF16)
        ym0 = YA.cur
        ckv = alloc(YA, [2, NT], BF16)
        kr = alloc(YA, [NT], BF16)
        ym = YA.cur
        norm_mod(xT, NT, TB, l, 0, 0, B32, 16, YA)
        YA.reset(ym)
        if stop_after == "norm":
            dump("hT", B32, [128, KC * (NT + 32)], BF16)
            return False
        if l > 0:
            STORE("sp", xsp_d.rearrange("(k p) n -> p k n", p=128), xT, "xsp")
        XA.reset()
        o = {"cq": cq, "ckv": ckv, "kr": kr, "q": alloc(XA, [NT], BF16), "k": alloc(XA, [NT], BF16),
             "vg": alloc(XA, [NT // 128, 512], BF16), "xp": alloc(XA, [2, NT + 16], F32)}
        ktok = alloc(XA, [NT // 128, 128], BF16)
        Lall = alloc(XA, [4, 544], F32)
        expBs = alloc(XA, [544], F32)
        xm = XA.cur
        inproj(l, B32, 16, NT, TB, True, o, YA)
        YA.reset(ym)
        if dbg and l == 0:
            dump("hT", B32, [128, KC * (NT + 32)], BF16)
            dump("qT", o["q"], [128, NT], BF16)
            dump("kT", o["k"], [128, NT], BF16)
            dump("vg", o["vg"], [128, (NT // 128) * 512], BF16)
            dump("cq", cq, [128, 3 * NT], BF16)
            dump("ckv", ckv, [128, 2 * NT], BF16)
            dump("kr", kr, [128, NT], BF16)
            dump("xp", o["xp"], [128, 2 * (NT + 16)])
        if stop_after in ("inproj", "ip1", "ip2", "ip3", "ip4"):
            return False
        STORE("sp", expK[l][:, :], kr[0:32, :], "expK%d" % l)
        COLL(expK[l], gK[l], "expK%d" % l, "gK%d" % l)
        kvc = kv_produce(l, ckv, NT, TB, False, YA)
        YA.reset(ym)
        ret_ktok(o["k"], ktok, NT)
        T = RetTmp(YA, NT // 128)
        ret_pass1(T, ktok, o["vg"], NT, expBs[:, 32:288], expBs[:, 288:544])
        for c in range(2):
            CP("act", expBs[:, c * 16:c * 16 + 8], o["xp"][:, c, 8:16])
            CP("act", expBs[:, c * 16 + 8:c * 16 + 16], o["xp"][:, c, NT:NT + 8])
        STORE("sp", expB[l], expBs, "expB%d" % l)
        COLL(expB[l], gB[l], "expB%d" % l, "gB%d" % l)
        LOAD("sp", Lall, gB[l].rearrange("(r p) n -> p r n", p=128), "gB%d" % l)
        sel = C("sel")
        for d in range(2):
            TS1("dve", Sin[:, d, :], sctx[:, d, :], coefc[:, d:d + 1], ALU.mult)
            for r in range(4):
                STT("dve", Sin[:, d, :], Lall[:, r, 32 + 256 * d:288 + 256 * d], coef[:, d, r:r + 1], Sin[:, d, :], ALU.mult, ALU.add)
        for c in range(2):
            for side, (dst, so, scol) in enumerate(((o["xp"][:, c, 0:8], 8, 20), (o["xp"][:, c, 8 + NT:16 + NT], 0, 24))):
                TS1("dve", dst, Lall[:, 0, c * 16 + so:c * 16 + so + 8], sel[:, scol:scol + 1], ALU.mult)
                for r in range(1, 4):
                    STT("dve", dst, Lall[:, r, c * 16 + so:c * 16 + so + 8], sel[:, scol + r:scol + r + 1], dst, ALU.mult, ALU.add)
        ret_pass2(T, o["q"], o["k"], ktok, o["vg"], NT, Sin, B32, 16, hooks={0: kvc[0], 5: kvc[1], 10: kvc[2], 15: kvc[3]})
        YA.reset(ym)
        XA.reset(xm)
        pool_mix(l, o["xp"], NT, TB, B32, 16, C("poolc", 2, 16), YA, XA)
        if dbg and l == 0:
            dump("mix_rp", B32, [128, KC * (NT + 32)], BF16)
        if stop_after == "retpool":
            return False
        YA.reset(ym0)
        XA.reset()
        wout = load_wout(l, YA)
        mla(l, cq, NT, TB, True, True, B32, 16, YA, XA)
        if dbg and l == 0:
            dump("mix", B32, [128, KC * (NT + 32)], BF16)
        if stop_after == "mla":
            return False
        XA.reset()
        src = xT_d if l == 0 else xsp_d
        for tb in range(4):
            LOAD("sp", xT[:, :, tb * 512:(tb + 1) * 512],
                 src.rearrange("(k p) n -> p k n", p=128)[:, :, tb * 512:(tb + 1) * 512], None if l == 0 else "xsp")
        out_proj(wout, B32, 16, NT, TB, xT, l, 0, order=[3, 0, 1, 2])
        if dbg and l == 0:
            dump("x_mid", xT, [128, KC * NT])
        if stop_after == "outproj":
            return False
        CA = Arena(YA.lo + 33280, ym0)
        YA.reset(ym0 + 16384)
        h2 = alloc(YA, [KC, NT + 2], BF16)
        stC = alloc(CA, [KC, 2], F32)
        gCs = alloc(CA, [4, 16], F32)
        hal = alloc(CA, [16], F32)
        def export_halo():
            CP("act", stC[:, :, 0], h2[:, :, 1])
            CP("act", stC[:, :, 1], h2[:, :, NT])
            STORE("sp", expC[l], stC.rearrange("p a b -> p (a b)"), "expC%d" % l)
            COLL(expC[l], gC[l], "expC%d" % l, "gC%d" % l)

        norm_mod(xT, NT, TB, l, 1, 0, h2, 1, CA, order=[3, 0, 1, 2], after={2: export_halo})
        FA = Arena(YA.lo, ym0 + 16384)
        LOAD("sp", gCs, gC[l].rearrange("(r p) n -> p r n", p=128), "gC%d" % l)
        g3 = gCs.rearrange("p r (k s) -> p r k s", s=2)
        h3 = hal.rearrange("p (k s) -> p k s", s=2)
        for side, (srcs, scol) in enumerate(((1, 20), (0, 24))):
            dst = h3[:, :, side]
            TS1("dve", dst, g3[:, 0, :, srcs], sel[:, scol:scol + 1], ALU.mult)
            for r in range(1, 4):
                STT("dve", dst, g3[:, r, :, srcs], sel[:, scol + r:scol + r + 1], dst, ALU.mult, ALU.add)
        CP("act", h2[:, :, 0], h3[:, :, 0])
        CP("act", h2[:, :, NT + 1], h3[:, :, 1])
        ffn(l, h2, NT, 1024, xT, 0, FA, depth=3)
        if dbg:
            dump("x_out%d" % l, xT, [128, KC * NT])
        return True

    def final_norm():
        YA.reset()
        NT, TB = TL, 512
        sq = [alloc(YA, [TB], BF16) for _ in range(2)]
        rs = [alloc(YA, [TB], F32) for _ in range(2)]
        ob = [alloc(YA, [KC, TB], F32) for _ in range(2)]
        fng = C("fng")
        for tb in range(NT // TB):
            cols = slice(tb * TB, (tb + 1) * TB)
            b = pbank()
            for kc in range(KC):
                s = sq[kc % 2]
                TT("pool", s, xT[:, kc, cols], xT[:, kc, cols], ALU.mult)
                MM(ps[:, b, 0:TB], ones_bf, s, start=(kc == 0), stop=(kc == KC - 1))
            r = rs[tb % 2]
            ACT(r, ps[:, b, 0:TB], AF.Sqrt, bias=epsc[:, 0:1], scale=1.0 / D)
            RCP(r, r)
            for kc in range(KC):
                STT("dve", ob[tb % 2][:, kc, :], xT[:, kc, cols], fng[:, kc:kc + 1], r, ALU.mult, ALU.mult)
            STORE("sp", outT_d.rearrange("(k p) n -> p k n", p=128)[:, :, cols], ob[tb % 2], "outT")

    ok = True
    for l in range(DEPTH):
        load_layer_consts(l)
        if not skip_ctx:
            ctx_layer(l)
        ok = latent_layer(l)
        if not ok:
            break
    if ok:
        final_norm()
    keys_out = ["outT"] + dumps
    S.add("sp", None, reads=keys_out, writes=[])
    S.emit(nc)
    return nc, dumps


_PROG = {}


def _get_prog(dbg=False, stop_after=None):
    key = (dbg, stop_after)
    if key not in _PROG:
        _PROG[key] = build_program(dbg, stop_after)
    return _PROG[key]


def make_in_maps(inp):
    inp = {k: np.asarray(v, np.float32) for k, v in inp.items()}
    shared = {k: np.ascontiguousarray(inp[k]) for k in
              ("w_in", "w_uq", "w_ukv", "pool_w", "w_out", "w_up", "w_down")}
    wm = inp["w_mod"].reshape(DEPTH, D, 6, D).transpose(0, 2, 1, 3).reshape(DEPTH * 6, D, D)
    ident = np.eye(128, dtype=np.float32)
    maps = []
    for core in range(NCORES):
        b, j = core // 4, core % 4
        cs, sn = rope_tables(j * TL, TL)
        m = dict(shared)
        m["xT"] = np.ascontiguousarray(inp["x"][b, j * TL:(j + 1) * TL, :].T)
        m["ctxT"] = np.ascontiguousarray(inp["ctx"][b].T)
        m["cpk"] = build_cpk(core, inp)
        m["w_mod_p"] = np.ascontiguousarray(wm[3 * j:3 * j + 3])
        m["cs"] = cs
        m["sn"] = sn
        m["ident"] = ident
        maps.append(m)
    return maps


def kernel(**inputs):
    nc, _ = _get_prog()
    maps = make_in_maps(inputs)
    res = run_bass_kernel_spmd(nc, maps, core_ids=list(range(NCORES)))
    out = np.empty((2, 4 * TL, D), np.float32)
    for core in range(NCORES):
        b, j = core // 4, core % 4
        out[b, j * TL:(j + 1) * TL, :] = np.asarray(res.results[core]["outT"]).T
    return out
```
